# Optimizing a Trainium2 kernel written in Bass

```python
import math
import jax, jax.numpy as jnp
from jax import lax
import numpy as np

D_MODEL = 1024
BATCH = 16
SEQ = 2048
DEPTH = 2

RET_HEADS = 4
RET_QK_W = D_MODEL // 2
RET_V_W = D_MODEL
RET_QK_DIM = RET_QK_W // RET_HEADS
RET_V_DIM = RET_V_W // RET_HEADS
RET_CHUNK = 128
ROPE_BASE = 10000.0
SB_HEADS = 8
SB_W = D_MODEL // 2
SB_HEAD_DIM = SB_W // SB_HEADS
SB_BLOCK = 128
SSM_W = D_MODEL // 2
SSM_GROUP = 16
SSM_GROUPS = SSM_W // SSM_GROUP
SSM_STATE = 64
DT_MIN = 1e-3
DT_MAX = 1e-1

EPS = 1e-6
IN_SIZES = (RET_QK_W, RET_QK_W, RET_V_W, RET_V_W,
            SB_W, SB_W, SB_W, SB_W,
            SSM_W, SSM_W,
            D_MODEL, D_MODEL, D_MODEL)
IN_W = sum(IN_SIZES)

kernel_name = "hybrid_retention_stickbreak_s5_gated"

F32 = jnp.float32


def rmsnorm(t, g):
    tf = t.astype(F32)
    return tf * lax.rsqrt(jnp.mean(tf * tf, axis=-1, keepdims=True) + EPS) * g.astype(F32)


def rope(t, cos, sin):
    half = t.shape[-1] // 2
    t1, t2 = t[..., :half], t[..., half:]
    return jnp.concatenate([t1 * cos - t2 * sin, t2 * cos + t1 * sin], axis=-1)


def retention(q, k, v, q_norm, k_norm, out_norm, cos, sin):
    B_, S_ = q.shape[:2]
    q = rope(rmsnorm(q, q_norm), cos, sin)
    k = rope(rmsnorm(k, k_norm), cos, sin) * (RET_QK_DIM ** -0.5)
    v = v.astype(F32)
    n = S_ // RET_CHUNK
    qc = q.reshape(B_, n, RET_CHUNK, RET_HEADS, RET_QK_DIM)
    kc = k.reshape(B_, n, RET_CHUNK, RET_HEADS, RET_QK_DIM)
    vc = v.reshape(B_, n, RET_CHUNK, RET_HEADS, RET_V_DIM)
    log_gamma = jnp.log1p(-jnp.exp2(-5.0 - jnp.arange(RET_HEADS, dtype=F32)))
    idx = jnp.arange(RET_CHUNK, dtype=F32)
    rel = idx[:, None] - idx[None, :]
    decay = jnp.where(rel >= 0, jnp.exp(jnp.maximum(rel, 0.0)[None] * log_gamma[:, None, None]), 0.0)
    scores = jnp.einsum('bnihd,bnjhd->bnhij', qc, kc) * decay[None, None]
    inner = jnp.einsum('bnhij,bnjhe->bnihe', scores, vc)
    k_decay = jnp.exp((RET_CHUNK - 1 - idx)[:, None] * log_gamma[None])
    q_decay = jnp.exp((idx + 1)[:, None] * log_gamma[None])
    chunk_kv = jnp.einsum('bnjhd,bnjhe->nbhde', kc * k_decay[:, :, None], vc)
    chunk_decay = jnp.exp(RET_CHUNK * log_gamma)[None, :, None, None]

    def step(state, kv):
        return state * chunk_decay + kv, state

    _, prev = lax.scan(step, jnp.zeros_like(chunk_kv[0]), chunk_kv)
    cross = jnp.einsum('bnihd,nbhde->bnihe', qc * q_decay[:, :, None], prev)
    out = (inner + cross).reshape(B_, S_, RET_HEADS, RET_V_DIM)
    out = rmsnorm(out, out_norm.reshape(RET_HEADS, RET_V_DIM))
    return out.reshape(B_, S_, RET_V_W)


def stick_breaking(q, k, v, q_norm, k_norm):
    B_, S_ = q.shape[:2]
    q = rmsnorm(q, q_norm)
    k = rmsnorm(k, k_norm)
    v = v.astype(F32)
    scale = SB_HEAD_DIM ** -0.5
    outs = []
    for blk in range(S_ // SB_BLOCK):
        start = blk * SB_BLOCK
        end = start + SB_BLOCK
        z = jnp.einsum('bqhd,bkhd->bhqk', q[:, start:end], k[:, :end]) * scale
        t_idx = start + jnp.arange(SB_BLOCK)
        s_idx = jnp.arange(end)
        causal = s_idx[None, :] < t_idx[:, None]
        log_keep = jnp.where(causal, jax.nn.log_sigmoid(-z), 0.0)
        after = lax.cumsum(log_keep, axis=3, reverse=True) - log_keep
        w = jnp.where(causal, jnp.exp(jax.nn.log_sigmoid(z) + after), 0.0)
        outs.append(jnp.einsum('bhqk,bkhd->bqhd', w, v[:, :end]))
    return jnp.concatenate(outs, axis=1).reshape(B_, S_, SB_W)


def _ssm_combine(left, right):
    a1r, a1i, b1r, b1i = left
    a2r, a2i, b2r, b2i = right
    return (a2r * a1r - a2i * a1i,
            a2r * a1i + a2i * a1r,
            a2r * b1r - a2i * b1i + b2r,
            a2r * b1i + a2i * b1r + b2i)


def s5_ssm(u, a_re, a_im, log_dt, b_re, b_im, c_re, c_im, d_skip, w_glu, b_glu):
    B_, S_ = u.shape[:2]
    u = u.astype(F32)
    ug = jnp.swapaxes(u.reshape(B_, S_, SSM_GROUPS, SSM_GROUP), 0, 1)
    a_re = a_re.astype(F32)
    a_im = a_im.astype(F32)
    dt = jnp.exp(log_dt.astype(F32))[:, None]
    mag = jnp.exp(dt * a_re)
    ab_re = mag * jnp.cos(dt * a_im)
    ab_im = mag * jnp.sin(dt * a_im)
    den = a_re * a_re + a_im * a_im
    nr = ab_re - 1.0
    coef_re = (nr * a_re + ab_im * a_im) / den
    coef_im = (ab_im * a_re - nr * a_im) / den
    b_re = b_re.astype(F32)
    b_im = b_im.astype(F32)
    bb_re = coef_re[..., None] * b_re - coef_im[..., None] * b_im
    bb_im = coef_re[..., None] * b_im + coef_im[..., None] * b_re
    bu_re = jnp.einsum('sbgm,gpm->sbgp', ug, bb_re)
    bu_im = jnp.einsum('sbgm,gpm->sbgp', ug, bb_im)
    a_seq_re = jnp.broadcast_to(ab_re[None, None], (S_, 1, SSM_GROUPS, SSM_STATE))
    a_seq_im = jnp.broadcast_to(ab_im[None, None], (S_, 1, SSM_GROUPS, SSM_STATE))
    _, _, h_re, h_im = lax.associative_scan(_ssm_combine, (a_seq_re, a_seq_im, bu_re, bu_im), axis=0)
    y = (jnp.einsum('sbgp,gmp->sbgm', h_re, c_re.astype(F32))
         - jnp.einsum('sbgp,gmp->sbgm', h_im, c_im.astype(F32)))
    y = jnp.swapaxes(y, 0, 1).reshape(B_, S_, SSM_W) + d_skip.astype(F32) * u
    y = jax.nn.gelu(y)
    return y * jax.nn.sigmoid(y @ w_glu.astype(F32) + b_glu.astype(F32))


def hybrid_layer(x, norm_g, w_in, ret_q_norm, ret_k_norm, ret_out_norm, sb_q_norm, sb_k_norm,
                 ssm_a_re, ssm_a_im, ssm_log_dt, ssm_b_re, ssm_b_im, ssm_c_re, ssm_c_im,
                 ssm_d, ssm_w_glu, ssm_b_glu, proj_a, proj_b, proj_c, w_out, cos, sin):
    B_, S_, _ = x.shape
    h = rmsnorm(x, norm_g).astype(x.dtype)
    proj = h @ w_in
    points = []
    acc = 0
    for sz in IN_SIZES[:-1]:
        acc += sz
        points.append(acc)
    (rq, rk, rv, rz, sq, sk, sv, sz_, cu, cz, ga, gb, gc) = jnp.split(proj, points, axis=-1)
    y_a = retention(rq.reshape(B_, S_, RET_HEADS, RET_QK_DIM),
                    rk.reshape(B_, S_, RET_HEADS, RET_QK_DIM),
                    rv.reshape(B_, S_, RET_HEADS, RET_V_DIM),
                    ret_q_norm, ret_k_norm, ret_out_norm, cos, sin) * jax.nn.silu(rz.astype(F32))
    y_b = stick_breaking(sq.reshape(B_, S_, SB_HEADS, SB_HEAD_DIM),
                         sk.reshape(B_, S_, SB_HEADS, SB_HEAD_DIM),
                         sv.reshape(B_, S_, SB_HEADS, SB_HEAD_DIM),
                         sb_q_norm, sb_k_norm) * jax.nn.silu(sz_.astype(F32))
    y_c = s5_ssm(cu, ssm_a_re, ssm_a_im, ssm_log_dt, ssm_b_re, ssm_b_im, ssm_c_re, ssm_c_im,
                 ssm_d, ssm_w_glu, ssm_b_glu) * jax.nn.silu(cz.astype(F32))
    merged = (jax.nn.sigmoid(ga.astype(F32)) * (y_a @ proj_a.astype(F32))
              + jax.nn.sigmoid(gb.astype(F32)) * (y_b @ proj_b.astype(F32))
              + jax.nn.sigmoid(gc.astype(F32)) * (y_c @ proj_c.astype(F32)))
    return x + (merged @ w_out.astype(F32)).astype(x.dtype)


def setup_inputs(seed: int = 0) -> dict:
    key = jax.random.key(seed)
    ks = jax.random.split(key, 24)
    L = DEPTH

    def nrm(k, shape, std):
        return jax.random.normal(k, shape, F32) * std

    def gain(k, shape):
        return 1.0 + 0.05 * jax.random.normal(k, shape, F32)

    n_idx = jnp.arange(SSM_STATE, dtype=F32)
    a_re = -0.5 + 0.01 * jax.random.normal(ks[8], (L, SSM_GROUPS, SSM_STATE), F32)
    a_im = math.pi * n_idx + 0.01 * jax.random.normal(ks[9], (L, SSM_GROUPS, SSM_STATE), F32)
    log_dt = jax.random.uniform(ks[10], (L, SSM_GROUPS), F32, math.log(DT_MIN), math.log(DT_MAX))
    return {
        "x": jax.random.normal(ks[0], (BATCH, SEQ, D_MODEL), F32),
        "norm_g": gain(ks[1], (L, D_MODEL)),
        "w_in": nrm(ks[2], (L, D_MODEL, IN_W), D_MODEL ** -0.5),
        "ret_q_norm": gain(ks[3], (L, RET_QK_DIM)),
        "ret_k_norm": gain(ks[4], (L, RET_QK_DIM)),
        "ret_out_norm": gain(ks[5], (L, RET_V_W)),
        "sb_q_norm": gain(ks[6], (L, SB_HEAD_DIM)),
        "sb_k_norm": gain(ks[7], (L, SB_HEAD_DIM)),
        "ssm_a_re": a_re,
        "ssm_a_im": a_im,
        "ssm_log_dt": log_dt,
        "ssm_b_re": nrm(ks[11], (L, SSM_GROUPS, SSM_STATE, SSM_GROUP), (2 * SSM_GROUP) ** -0.5),
        "ssm_b_im": nrm(ks[12], (L, SSM_GROUPS, SSM_STATE, SSM_GROUP), (2 * SSM_GROUP) ** -0.5),
        "ssm_c_re": nrm(ks[13], (L, SSM_GROUPS, SSM_GROUP, SSM_STATE), SSM_STATE ** -0.5),
        "ssm_c_im": nrm(ks[14], (L, SSM_GROUPS, SSM_GROUP, SSM_STATE), SSM_STATE ** -0.5),
        "ssm_d": nrm(ks[15], (L, SSM_W), 1.0),
        "ssm_w_glu": nrm(ks[16], (L, SSM_W, SSM_W), SSM_W ** -0.5),
        "ssm_b_glu": nrm(ks[17], (L, SSM_W), 0.01),
        "proj_a": nrm(ks[18], (L, RET_V_W, D_MODEL), RET_V_W ** -0.5),
        "proj_b": nrm(ks[19], (L, SB_W, D_MODEL), SB_W ** -0.5),
        "proj_c": nrm(ks[20], (L, SSM_W, D_MODEL), SSM_W ** -0.5),
        "w_out": nrm(ks[21], (L, D_MODEL, D_MODEL), D_MODEL ** -0.5),
    }


def reference(x, norm_g, w_in, ret_q_norm, ret_k_norm, ret_out_norm, sb_q_norm, sb_k_norm,
              ssm_a_re, ssm_a_im, ssm_log_dt, ssm_b_re, ssm_b_im, ssm_c_re, ssm_c_im,
              ssm_d, ssm_w_glu, ssm_b_glu, proj_a, proj_b, proj_c, w_out):
    S_ = x.shape[1]
    half = RET_QK_DIM // 2
    inv_freq = ROPE_BASE ** (-jnp.arange(half, dtype=F32) / half)
    ang = jnp.arange(S_, dtype=F32)[:, None] * inv_freq[None, :]
    cos = jnp.cos(ang)[:, None, :]
    sin = jnp.sin(ang)[:, None, :]
    for l in range(DEPTH):
        x = hybrid_layer(x, norm_g[l], w_in[l], ret_q_norm[l], ret_k_norm[l], ret_out_norm[l],
                         sb_q_norm[l], sb_k_norm[l], ssm_a_re[l], ssm_a_im[l], ssm_log_dt[l],
                         ssm_b_re[l], ssm_b_im[l], ssm_c_re[l], ssm_c_im[l], ssm_d[l],
                         ssm_w_glu[l], ssm_b_glu[l], proj_a[l], proj_b[l], proj_c[l], w_out[l],
                         cos, sin)
    return x
```

```python
import contextlib
import math
import numpy as np
import concourse.bass as bass
import concourse.mybir as mybir
from concourse.bass_utils import run_bass_kernel_spmd

F32 = mybir.dt.float32
BF16 = mybir.dt.bfloat16
I32 = mybir.dt.int32
U8 = mybir.dt.uint8
AF = mybir.ActivationFunctionType
ALU = mybir.AluOpType
AX = mybir.AxisListType

N_DMA_SEMS = 24
SAME_ENGINE_RAW_SYNC = True


class Buf:
    __slots__ = ("name", "w", "r")

    def __init__(self, name=""):
        self.name = name
        self.w = None
        self.r = []


class Op:
    __slots__ = ("eng", "fn", "deps", "dma", "sig", "sem", "val", "prewait", "asyn", "raw")

    def __init__(self, eng, fn, dma, asyn=False):
        self.eng = eng
        self.fn = fn
        self.dma = dma
        self.asyn = asyn
        self.raw = set()
        self.deps = set()
        self.sig = False
        self.sem = None
        self.val = None
        self.prewait = None


class Prog:
    ENGS = ("pe", "act", "dve", "pool", "sp")

    def __init__(self, nc):
        self.nc = nc
        self.ops = []
        self.last = {e: None for e in self.ENGS}
        self.pending_dma = []

    def op(self, eng, fn, reads=(), writes=(), dma=False, asyn=False):
        o = Op(eng, fn, dma, asyn)
        oid = len(self.ops)
        for b in reads:
            if b.w is not None:
                o.deps.add(b.w)
                o.raw.add(b.w)
        for b in writes:
            if b.w is not None:
                o.deps.add(b.w)
            o.deps.update(b.r)
        for b in reads:
            b.r.append(oid)
        for b in writes:
            b.w = oid
            b.r = []
        self.ops.append(o)
        if fn is not None:
            self.last[eng] = oid
            if dma:
                self.pending_dma.append(oid)
        return oid

    def dma(self, eng, out, in_, reads=(), writes=(), **kw):
        return self.op(eng, lambda e: e.dma_start(out=out, in_=in_, **kw), reads, writes, dma=True)

    def barrier(self):
        deps = set(x for x in self.last.values() if x is not None)
        deps.update(self.pending_dma)
        self.pending_dma = []
        for e in self.ENGS:
            o = Op(e, None, False)
            o.deps = set(deps)
            self.ops.append(o)

    def emit(self):
        nc = self.nc
        ops = self.ops

        def needs_wait(o, p, d):
            if p.fn is None:
                return False
            if p.eng == o.eng and not p.dma and not p.asyn:
                return SAME_ENGINE_RAW_SYNC and o.eng != "pe" and d in o.raw
            return True

        for o in ops:
            for d in o.deps:
                if needs_wait(o, ops[d], d):
                    ops[d].sig = True
        with contextlib.ExitStack() as st:
            esem = {e: st.enter_context(nc.semaphore("s_" + e)) for e in self.ENGS}
            dsem = [st.enter_context(nc.semaphore("d%d" % i)) for i in range(N_DMA_SEMS)]
            cnt = {e: 0 for e in self.ENGS}
            nd = 0
            for o in ops:
                if o.fn is None:
                    continue
                if o.dma:
                    s = nd % N_DMA_SEMS
                    k = nd // N_DMA_SEMS
                    o.sem = dsem[s]
                    o.val = 16 * (k + 1)
                    o.prewait = (dsem[s], 16 * k) if k > 0 else None
                    o.sig = True
                    nd += 1
                elif o.sig:
                    cnt[o.eng] += 1
                    o.sem = esem[o.eng]
                    o.val = cnt[o.eng]
            streams = {e: [] for e in self.ENGS}
            for o in ops:
                streams[o.eng].append(o)
            block = st.enter_context(nc.Block())

            def run(engname, e):
                waited = {}
                for o in streams[engname]:
                    best = {}
                    if o.prewait is not None:
                        best[o.prewait[0].num] = o.prewait
                    for d in o.deps:
                        p = ops[d]
                        if not needs_wait(o, p, d):
                            continue
                        cur = best.get(p.sem.num)
                        if cur is None or p.val > cur[1]:
                            best[p.sem.num] = (p.sem, p.val)
                    for sn in sorted(best):
                        s, v = best[sn]
                        if waited.get(sn, 0) >= v:
                            continue
                        e.wait_ge(s, v)
                        waited[sn] = v
                    if o.fn is None:
                        continue
                    ins = o.fn(e)
                    if o.sig:
                        ins.then_inc(o.sem, 16 if o.dma else 1)

            @block.tensor
            def _(e):
                run("pe", e)

            @block.scalar
            def _(e):
                run("act", e)

            @block.vector
            def _(e):
                run("dve", e)

            @block.gpsimd
            def _(e):
                run("pool", e)

            @block.sync
            def _(e):
                run("sp", e)


D = 1024
NKC = 8
NL = 2
EPS = 1e-6
C_RQ, C_RK, C_RV, C_RZ = 0, 512, 1024, 2048
C_SQ, C_SK, C_SV, C_SZ = 3072, 3584, 4096, 4608
C_CU, C_CZ = 5120, 5632
C_GA, C_GB, C_GC = 6144, 7168, 8192
IN_W = 9216
NEG = -30000.0
TWO_PI = 2.0 * math.pi
CW1 = 6.28125
CW2 = float(np.float32(TWO_PI - CW1))
CW3 = float(TWO_PI - CW1 - CW2)
PI_CL = 3.1415925

PK_G = 0
PK_ARE = 8
PK_AIM = 24
PK_LDT = 40
PK_D = 56
PK_BG = 60
PK_SBQ = 64
PK_SBK = 65
NPK = 66
PA_Q = 0
PA_K = 128
PA_O = 256
NPA = 1280
NPC_ = 1024

CS_ID = 0
CS_MASKT = 128
CS_TRIN = 256
CS_ONESN = 384
CS_ONESB = 512
CS_NEGM = 640
CS_DECQ = 640 + 2048
CS_DECK = CS_DECQ + 4
CS_POS = CS_DECK + 4
CS_INVF = CS_POS + 16
CS_EV0 = CS_INVF + 64
CS_EV1 = CS_EV0 + 9
CS_BLK = CS_EV1 + 8
NCST = CS_BLK + 128


def make_consts():
    c = np.zeros((128, NCST), np.float32)
    idx = np.arange(128)
    c[:, CS_ID:CS_ID + 128] = np.eye(128, dtype=np.float32)
    c[:, CS_MASKT:CS_MASKT + 128] = (idx[:, None] <= idx[None, :]).astype(np.float32)
    c[:, CS_TRIN:CS_TRIN + 128] = -(idx[:, None] >= idx[None, :]).astype(np.float32)
    c[:, CS_ONESN:CS_ONESN + 128] = -1.0
    blk = (idx[:, None] // 64 == idx[None, :] // 64).astype(np.float32)
    c[:, CS_ONESB:CS_ONESB + 128] = blk
    t = np.arange(512)
    for o in range(4):
        s = o * 128 + idx
        c[:, CS_NEGM + o * 512:CS_NEGM + (o + 1) * 512] = np.where(s[:, None] >= t[None, :], NEG, 0.0)
    lg = np.log1p(-np.exp2(-5.0 - np.arange(4, dtype=np.float64)))
    c[:, CS_DECQ:CS_DECQ + 4] = np.exp((idx[:, None] + 1) * lg[None, :])
    c[:, CS_DECK:CS_DECK + 4] = np.exp(-(idx[:, None] + 1) * lg[None, :]) * (128.0 ** -0.5)
    c[:, CS_POS:CS_POS + 16] = (np.arange(16)[None, :] * 128 + idx[:, None]).astype(np.float32)
    half = 64
    invf = (np.float32(10000.0) ** (-np.arange(half, dtype=np.float32) / np.float32(half))).astype(np.float32)
    c[:, CS_INVF:CS_INVF + 64] = invf[None, :]
    c[:, CS_EV0:CS_EV0 + 9] = np.arange(9, dtype=np.float32)[None, :]
    c[:, CS_EV1:CS_EV1 + 8] = (8.0 * 2.0 ** np.arange(8))[None, :]
    c[:, CS_BLK:CS_BLK + 128] = (idx[:, None] // 32 == idx[None, :] // 32).astype(np.float32)
    return c


G128 = [float(np.exp(128.0 * np.log1p(-np.exp2(-5.0 - h)))) for h in range(4)]

KB = 1024
OFF_HT = 0
OFF_YA = 32 * KB
OFF_YB = 64 * KB
OFF_YC = 80 * KB
OFF_CST = 96 * KB
OFF_WS = 112 * KB
OFF_WK = 136 * KB
TAB_BYTES = 42 * KB
ARENA = 207 * KB
WK_SIZE = ARENA - OFF_WK


def _dsz(dt):
    return {F32: 4, BF16: 2, I32: 4, U8: 1}[dt]


class Builder:
    def __init__(self, S, NSEQ=2, layers=(0, 1), dbg=()):
        self.S = S
        self.NSEQ = NSEQ
        self.layers = tuple(layers)
        self.NCH = S // 128
        self.NPC = S // 512
        self.NC8 = S // 8
        self.LEV = int(round(math.log2(self.NC8)))
        self.PAD = self.NC8 // 2
        self.dbg = set(dbg)
        self.dbg_out = {}

    def V(self, off, dt, shape):
        n = 1
        for s in shape:
            n *= s
        nb = n * _dsz(dt)
        assert off % 4 == 0 and off + nb <= ARENA, (off, nb)
        ap = self.arena[:, off:off + nb].bitcast(dt)
        if len(shape) == 2:
            ap = ap.rearrange("p (a b) -> p a b", a=shape[0])
        elif len(shape) == 3:
            ap = ap.rearrange("p (a b c) -> p a b c", a=shape[0], b=shape[1])
        elif len(shape) == 4:
            ap = ap.rearrange("p (a b c d) -> p a b c d", a=shape[0], b=shape[1], c=shape[2])
        return ap

    class Alloc:
        def __init__(self, b, off, size):
            self.b, self.off, self.end, self.cur = b, off, off + size, off

        def get(self, dt, shape):
            n = 1
            for s in shape:
                n *= s
            nb = (n * _dsz(dt) + 63) // 64 * 64
            assert self.cur + nb <= self.end, ("alloc overflow", self.cur + nb - self.end)
            v = self.b.V(self.cur, dt, shape)
            self.cur += nb
            return v

    def act(self, out, in_, func, r, w, **kw):
        self.P.op("act", lambda e: e.activation(out, in_, func, **kw), r, w, asyn=("accum_out" in kw))

    def tt(self, eng, out, a, b, op, r, w):
        self.P.op(eng, lambda e: e.tensor_tensor(out, a, b, op), r, w)

    def ts(self, eng, out, a, s1, s2, op0, op1, r, w):
        self.P.op(eng, lambda e: e.tensor_scalar(out, a, s1, s2, op0, op1), r, w)

    def stt(self, out, in0, scalar, in1, op0, op1, r, w, accum_out=None):
        if accum_out is None:
            self.P.op("dve", lambda e: e.scalar_tensor_tensor(out, in0, scalar, in1, op0, op1), r, w)
        else:
            self.P.op("dve", lambda e: e.scalar_tensor_tensor(out, in0, scalar, in1, op0, op1, accum_out=accum_out), r, w, asyn=True)

    def cp(self, eng, out, in_, r, w):
        if eng == "act":
            self.P.op("act", lambda e: e.activation(out, in_, AF.Copy), r, w)
        else:
            self.P.op(eng, lambda e: e.tensor_copy(out, in_), r, w)

    def mm(self, out, lhsT, rhs, start, stop, r, w, **kw):
        self.P.op("pe", lambda e: e.matmul(out, lhsT, rhs, start=start, stop=stop, **kw), r, w)

    def tr(self, out, in_, ident, r, w):
        self.P.op("pe", lambda e: e.transpose(out, in_, ident), r, w)

    def memset(self, eng, ap, val, w):
        self.P.op(eng, lambda e: e.memset(ap, val), (), w)

    def dump(self, name, ap, shape, dt, r):
        if name not in self.dbg:
            return
        t = self.nc.dram_tensor("dbg_" + name, list(shape), dt, kind="ExternalOutput").ap()
        self.dbg_out[name] = t
        self.P.barrier()
        self.P.dma("sp", t, ap, reads=r, writes=[self.b_out])
        self.P.barrier()

    def wslot(self):
        i = self.ws_i % 3
        self.ws_i += 1
        return OFF_WS + i * 8 * KB, self.b_ws[i]

    def wload(self, parts):
        if not parts:
            return [], Buf()
        off, buf = self.wslot()
        views = []
        cur = off
        for (dv, kc, n) in parts:
            v = self.V(cur, BF16, (kc, n))
            cur += kc * n * 2
            assert cur <= off + 8 * KB
            self.P.dma("pool", v, dv, writes=[buf])
            views.append(v)
        return views, buf

    def run_jobs(self, jobs, depth=2):
        loaded = {}
        n = len(jobs)
        for i in range(min(depth, n)):
            loaded[i] = self.wload(jobs[i][0])
        for i in range(n):
            if i + depth < n:
                loaded[i + depth] = self.wload(jobs[i + depth][0])
            views, buf = loaded.pop(i)
            jobs[i][1](views, buf)

    def win(self, l, c0, n):
        return self.d_win[l].rearrange("(kc p) n -> p kc n", p=128)[:, :, c0:c0 + n]

    def emit_sin(self, out, x, tmpf, tmpi, tmpr, shift, r, w, bt):
        self.ts("dve", tmpf, x, shift, 1.0 / TWO_PI, ALU.add, ALU.mult, r, [bt])
        self.cp("dve", tmpi, tmpf, [bt], [bt])
        self.cp("dve", tmpf, tmpi, [bt], [bt])
        self.ts("dve", tmpr, x, shift, None, ALU.add, ALU.bypass, r, [bt])
        self.stt(tmpr, tmpf, -CW1, tmpr, ALU.mult, ALU.add, [bt], [bt])
        self.stt(tmpr, tmpf, -CW2, tmpr, ALU.mult, ALU.add, [bt], [bt])
        self.stt(tmpr, tmpf, -CW3, tmpr, ALU.mult, ALU.add, [bt], [bt])
        self.ts("dve", tmpr, tmpr, PI_CL, -PI_CL, ALU.min, ALU.max, [bt], [bt])
        self.act(out, tmpr, AF.Sin, [bt], w)

    def build(self):
        S, NSEQ = self.S, self.NSEQ
        nc = bass.Bass("TRN2", target_bir_lowering=False)
        self.nc = nc
        dt = nc.dram_tensor
        self.d_x = dt("x", [NSEQ, S, D], F32, kind="ExternalInput").ap()
        self.d_win = dt("w_in", [NL, D, IN_W], F32, kind="ExternalInput").ap()
        self.d_pa = dt("proj_a", [NL, 1024, D], F32, kind="ExternalInput").ap()
        self.d_pb = dt("proj_b", [NL, 512, D], F32, kind="ExternalInput").ap()
        self.d_pc = dt("proj_c", [NL, 512, D], F32, kind="ExternalInput").ap()
        self.d_wout = dt("w_out", [NL, D, D], F32, kind="ExternalInput").ap()
        self.d_wglu = dt("w_glu", [NL, 512, 512], F32, kind="ExternalInput").ap()
        self.d_pk = dt("pk", [NL, 128, NPK], F32, kind="ExternalInput").ap()
        self.d_pka = dt("pka", [NL, 128, NPA], F32, kind="ExternalInput").ap()
        self.d_pkc = dt("pkc", [NL, 128, NPC_], F32, kind="ExternalInput").ap()
        self.d_cst = dt("cst", [128, NCST], F32, kind="ExternalInput").ap()
        self.d_xs = dt("xs", [NSEQ, S, D], F32, kind="Internal").ap()
        self.d_tabs = dt("tabs", [128, TAB_BYTES], U8, kind="Internal").ap()
        self.b_dtabs = Buf("dtabs")
        self.d_out = dt("out", [NSEQ, S, D], F32, kind="ExternalOutput").ap()

        with contextlib.ExitStack() as st:
            self.arena = st.enter_context(nc.sbuf_tensor("arena", [128, ARENA], U8))
            self.ps2 = [st.enter_context(nc.psum_tensor("ps%d" % i, [128, 1024], F32)) for i in range(4)]
            self.ps = [self.ps2[j // 2][:, (j % 2) * 512:(j % 2 + 1) * 512] for j in range(8)]
            self.b_ps = [Buf("ps%d" % i) for i in range(8)]
            self.P = Prog(nc)
            self.b_ws = [Buf("ws%d" % i) for i in range(3)]
            self.ws_i = 0
            self.b_out = Buf("out")
            self.b_hT = Buf("hT")
            self.b_ya = Buf("ya")
            self.b_yb = Buf("yb")
            self.b_yc = Buf("yc")
            self.b_cst = Buf("cst")
            self.b_xs = [Buf("xs%d" % s) for s in range(NSEQ)]
            self.hT = self.V(OFF_HT, BF16, (8, S))
            self.yaT = self.V(OFF_YA, BF16, (8, S))
            self.ybT = self.V(OFF_YB, BF16, (4, S))
            self.ycT = self.V(OFF_YC, BF16, (4, S))
            ca = Builder.Alloc(self, OFF_CST, 16 * KB)
            self.id_bf = ca.get(BF16, (128,))
            self.trin_bf = ca.get(BF16, (128,))
            self.onesn_bf = ca.get(BF16, (128,))
            self.negm_bf = ca.get(BF16, (4, 512))
            self.id_f = ca.get(F32, (128,))
            self.maskT = ca.get(F32, (128,))
            self.onesb = ca.get(F32, (128,))
            self.dec = ca.get(F32, (8,))
            self.ev = ca.get(F32, (17,))
            self.cosT = ca.get(F32, (16, 64))
            self.sinT = ca.get(F32, (16, 64))
            self.neghalf = ca.get(F32, (1,))
            self.epsc = ca.get(F32, (1,))
            self.ln8c = ca.get(F32, (1,))
            self.onec = ca.get(F32, (1,))
            self.gc1 = ca.get(F32, (1,))
            self.pk = ca.get(F32, (NPK,))

            self.init_consts()
            for l in self.layers:
                self.P.barrier()
                self.P.dma("sp", self.pk, self.d_pk[l], writes=[self.b_cst])
                self.P.barrier()
                for s in range(NSEQ):
                    self.layer(l, s)
            self.P.barrier()
            self.P.emit()
        return nc

    def init_consts(self):
        P = self.P
        wa = Builder.Alloc(self, OFF_WK, WK_SIZE)
        craw = wa.get(F32, (NCST,))
        b_raw = Buf("craw")
        b_t = Buf("ctmp")
        bc = self.b_cst
        P.dma("sp", craw, self.d_cst, writes=[b_raw])
        self.cp("dve", self.id_bf, craw[:, CS_ID:CS_ID + 128], [b_raw], [bc])
        self.cp("dve", self.trin_bf, craw[:, CS_TRIN:CS_TRIN + 128], [b_raw], [bc])
        self.cp("dve", self.onesn_bf, craw[:, CS_ONESN:CS_ONESN + 128], [b_raw], [bc])
        self.cp("dve", self.negm_bf, craw[:, CS_NEGM:CS_NEGM + 2048].rearrange("p (a b) -> p a b", a=4), [b_raw], [bc])
        self.cp("dve", self.id_f, craw[:, CS_ID:CS_ID + 128], [b_raw], [bc])
        self.cp("dve", self.maskT, craw[:, CS_MASKT:CS_MASKT + 128], [b_raw], [bc])
        self.cp("dve", self.onesb, craw[:, CS_ONESB:CS_ONESB + 128], [b_raw], [bc])
        self.cp("dve", self.dec, craw[:, CS_DECQ:CS_DECQ + 8], [b_raw], [bc])
        self.cp("dve", self.ev, craw[:, CS_EV0:CS_EV0 + 17], [b_raw], [bc])
        self.memset("dve", self.neghalf, -0.5, [bc])
        self.memset("dve", self.epsc, EPS, [bc])
        self.memset("dve", self.ln8c, math.log(0.125), [bc])
        self.memset("dve", self.onec, 1.0, [bc])
        self.memset("dve", self.gc1, 1.5957691216057308, [bc])
        ang = wa.get(F32, (16, 64))
        tf = wa.get(F32, (16, 64))
        ti = wa.get(I32, (16, 64))
        trr = wa.get(F32, (16, 64))
        pos = craw[:, CS_POS:CS_POS + 16]
        invf = craw[:, CS_INVF:CS_INVF + 64]
        self.tt("dve", ang, pos.unsqueeze(2).to_broadcast([128, 16, 64]),
                invf.unsqueeze(1).to_broadcast([128, 16, 64]), ALU.mult, [b_raw], [b_t])
        b_tt = Buf("sin_tmp")
        self.emit_sin(self.sinT, ang, tf, ti, trr, 0.0, [b_t], [bc], b_tt)
        self.emit_sin(self.cosT, ang, tf, ti, trr, math.pi / 2, [b_t], [bc], b_tt)
        P.barrier()

    def layer(self, l, s):
        first = (l == self.layers[0])
        last = (l == self.layers[-1])
        xsrc = self.d_x[s] if first else self.d_xs[s]
        xdst = self.d_out[s] if last else self.d_xs[s]
        self.phase0(l, s, xsrc)
        self.P.barrier()
        self.dump("hT", self.hT, (128, 8, self.S), BF16, [self.b_hT])
        if "only0" in self.dbg:
            return
        self.phaseC(l, s)
        self.P.barrier()
        self.dump("ycT", self.ycT, (128, 4, self.S), BF16, [self.b_yc])
        if "onlyC" in self.dbg:
            return
        self.phaseA(l, s)
        self.P.barrier()
        self.dump("yaT", self.yaT, (128, 8, self.S), BF16, [self.b_ya])
        self.phaseB(l, s)
        self.P.barrier()
        self.dump("ybT", self.ybT, (128, 4, self.S), BF16, [self.b_yb])
        self.phaseF(l, s, xsrc, xdst, last)
        self.P.barrier()

    def phase0(self, l, s, xsrc):
        P = self.P
        wa = Builder.Alloc(self, OFF_WK, WK_SIZE)
        NS0 = 6
        xin = [wa.get(F32, (D,)) for _ in range(NS0)]
        hb = [wa.get(BF16, (D,)) for _ in range(NS0)]
        junk = wa.get(BF16, (D,))
        ss = [wa.get(F32, (1,)) for _ in range(NS0)]
        ms = [wa.get(F32, (1,)) for _ in range(NS0)]
        rs = [wa.get(F32, (1,)) for _ in range(NS0)]
        b_x = [Buf() for _ in range(NS0)]
        b_hb = [Buf() for _ in range(NS0)]
        b_s = [Buf() for _ in range(NS0)]
        b_j = Buf()
        gT = self.pk[:, PK_G:PK_G + 8]
        def p0_s1(c):
            k = c % NS0
            P.dma("sp" if c % 2 == 0 else "pool", xin[k], xsrc[c * 128:(c + 1) * 128, :], reads=[self.b_xs[s]], writes=[b_x[k]])
            self.stt(junk, xin[k], 1.0, xin[k], ALU.mult, ALU.mult, [b_x[k]], [b_j, b_s[k]], accum_out=ss[k])
            self.ts("dve", ms[k], ss[k], 1.0 / D, EPS, ALU.mult, ALU.add, [b_s[k]], [b_s[k]])
            self.tt("pool", rs[k], ms[k], self.neghalf, ALU.pow, [b_s[k], self.b_cst], [b_s[k]])

        def p0_s2(c):
            k = c % NS0
            pb = 4 + (c % 4)
            psT = self.ps[pb][:].bitcast(BF16)
            self.act(hb[k], xin[k], AF.Copy, [b_x[k], b_s[k]], [b_hb[k]], scale=rs[k])
            for kc in range(8):
                self.tr(psT[:, kc * 128:(kc + 1) * 128], hb[k][:, kc * 128:(kc + 1) * 128], self.id_bf,
                        [b_hb[k], self.b_cst], [self.b_ps[pb]])
            self.tt("dve", self.hT[:, :, c * 128:(c + 1) * 128],
                    psT[:, 0:1024].rearrange("p (a b) -> p a b", a=8),
                    gT.unsqueeze(2).to_broadcast([128, 8, 128]), ALU.mult,
                    [self.b_ps[pb], self.b_cst], [self.b_hT])

        LA = 5
        for c in range(min(LA, self.NCH)):
            p0_s1(c)
        for c in range(self.NCH):
            if c + LA < self.NCH:
                p0_s1(c + LA)
            p0_s2(c)

    def phaseF(self, l, s, xsrc, xdst, to_out):
        P = self.P
        S = self.S
        wa = Builder.Alloc(self, OFF_WK, WK_SIZE)
        mT = wa.get(BF16, (8, S))
        b_m = Buf("mT")
        sg = [[wa.get(F32, (512,)) for _ in range(3)] for _ in range(2)]
        b_sg = [[Buf() for _ in range(3)] for _ in range(2)]
        mm_ = [wa.get(F32, (512,)) for _ in range(2)]
        b_mm = [Buf() for _ in range(2)]
        wo = wa.get(BF16, (8, D))
        b_wo = Buf("w_out")
        wov = self.d_wout[l].rearrange("(kc p) n -> p kc n", p=128)

        def load_wout():
            for hh in range(2):
                P.dma("pool", wo[:, :, hh * 512:(hh + 1) * 512], wov[:, :, hh * 512:(hh + 1) * 512], writes=[b_wo])
        pav = lambda l_, f: self.d_pa[l_].rearrange("(kc p) n -> p kc n", p=128)[:, :, f * 128:(f + 1) * 128]
        pbv = lambda l_, f: self.d_pb[l_].rearrange("(kc p) n -> p kc n", p=128)[:, :, f * 128:(f + 1) * 128]
        pcv = lambda l_, f: self.d_pc[l_].rearrange("(kc p) n -> p kc n", p=128)[:, :, f * 128:(f + 1) * 128]
        jobs = []
        state = {}
        cnt = [0]

        def gates_fn(fo):
            def f(views, buf):
                state["g"] = (views, buf)
                if fo == 1:
                    load_wout()
            return f

        def proj_fn(fo):
            def f(views, buf):
                gv, gb_ = state["g"]
                pa_, pb_, pc_ = views
                for pc in range(self.NPC):
                    k = cnt[0] % 2
                    cnt[0] += 1
                    sl = slice(pc * 512, (pc + 1) * 512)
                    gbank = [self.rot("F_g", [0, 1, 2, 3]) for _ in range(3)]
                    pbank = [self.rot("F_p", [4, 5, 6, 7]) for _ in range(3)]
                    for gi in range(3):
                        for kc in range(8):
                            self.mm(self.ps[gbank[gi]][:], gv[gi][:, kc, :], self.hT[:, kc, sl], kc == 0, kc == 7,
                                    [gb_, self.b_hT], [self.b_ps[gbank[gi]]])
                    for kc in range(8):
                        self.mm(self.ps[pbank[0]][:], pa_[:, kc, :], self.yaT[:, kc, sl], kc == 0, kc == 7,
                                [buf, self.b_ya], [self.b_ps[pbank[0]]])
                    for kc in range(4):
                        self.mm(self.ps[pbank[1]][:], pb_[:, kc, :], self.ybT[:, kc, sl], kc == 0, kc == 3,
                                [buf, self.b_yb], [self.b_ps[pbank[1]]])
                    for kc in range(4):
                        self.mm(self.ps[pbank[2]][:], pc_[:, kc, :], self.ycT[:, kc, sl], kc == 0, kc == 3,
                                [buf, self.b_yc], [self.b_ps[pbank[2]]])
                    for gi in range(3):
                        self.act(sg[k][gi], self.ps[gbank[gi]][:], AF.Sigmoid, [self.b_ps[gbank[gi]]], [b_sg[k][gi]])
                    self.tt("dve", mm_[k], sg[k][0], self.ps[pbank[0]][:], ALU.mult, [b_sg[k][0], self.b_ps[pbank[0]]], [b_mm[k]])
                    self.tt("dve", sg[k][1], sg[k][1], self.ps[pbank[1]][:], ALU.mult, [b_sg[k][1], self.b_ps[pbank[1]]], [b_sg[k][1]])
                    self.tt("dve", sg[k][2], sg[k][2], self.ps[pbank[2]][:], ALU.mult, [b_sg[k][2], self.b_ps[pbank[2]]], [b_sg[k][2]])
                    self.tt("dve", mm_[k], mm_[k], sg[k][1], ALU.add, [b_mm[k], b_sg[k][1]], [b_mm[k]])
                    self.tt("dve", mT[:, fo, sl], mm_[k], sg[k][2], ALU.add, [b_mm[k], b_sg[k][2]], [b_m])
            return f

        for fo in range(8):
            jobs.append(([(self.win(l, C_GA + fo * 128, 128), 8, 128), (self.win(l, C_GB + fo * 128, 128), 8, 128),
                          (self.win(l, C_GC + fo * 128, 128), 8, 128)], gates_fn(fo)))
            jobs.append(([(pav(l, fo), 8, 128), (pbv(l, fo), 4, 128), (pcv(l, fo), 4, 128)], proj_fn(fo)))
        self.run_jobs(jobs, depth=1)
        if "mT" in self.dbg:
            P.barrier()
            self.dump("mT", mT, (128, 8, S), BF16, [b_m])
        for e_ in ("pool", "dve"):
            P.op(e_, None, writes=[self.b_hT])
        NS2 = 4
        xin = [self.V(OFF_HT + i * 4 * KB, F32, (D,)) for i in range(NS2)]
        res = [self.V(OFF_HT + 16 * KB + i * 4 * KB, F32, (D,)) for i in range(NS2)]
        b_x = [Buf() for _ in range(NS2)]
        b_r = [Buf() for _ in range(NS2)]
        for c in range(self.NCH):
            k = c % NS2
            P.dma("pool", xin[k], xsrc[c * 128:(c + 1) * 128, :], reads=[self.b_xs[s]], writes=[b_x[k]])
            banks = [self.rot("F2", [0, 1, 2, 3, 4, 5, 6, 7]) for _ in range(2)]
            for hh in range(2):
                for kc in range(8):
                    self.mm(self.ps[banks[hh]][:], mT[:, kc, c * 128:(c + 1) * 128], wo[:, kc, hh * 512:(hh + 1) * 512],
                            kc == 0, kc == 7, [b_m, b_wo], [self.b_ps[banks[hh]]])
            for hh in range(2):
                self.tt("dve", res[k][:, hh * 512:(hh + 1) * 512], self.ps[banks[hh]][:], xin[k][:, hh * 512:(hh + 1) * 512],
                        ALU.add, [self.b_ps[banks[hh]], b_x[k]], [b_r[k]])
            wr = [Buf()]
            P.dma("sp" if c % 2 == 0 else "act", xdst[c * 128:(c + 1) * 128, :], res[k], reads=[b_r[k]], writes=wr)

    def rot(self, key, banks):
        d = self.__dict__.setdefault("_rot", {})
        i = d.get(key, 0)
        d[key] = i + 1
        return banks[i % len(banks)]

    def phaseA(self, l, s):
        P = self.P
        S, NCH, NPC = self.S, self.NCH, self.NPC
        NG = NCH // 4
        wa = Builder.Alloc(self, OFF_WK, WK_SIZE)
        pka = wa.get(F32, (NPA,))
        b_pka = Buf("pka")
        P.dma("sp", pka, self.d_pka[l], writes=[b_pka])
        gq = pka[:, PA_Q:PA_Q + 128]
        gk = pka[:, PA_K:PA_K + 128]
        go = pka[:, PA_O:PA_O + 1024]
        qT = wa.get(BF16, (2, S))
        kT = wa.get(BF16, (2, S))
        ktok = wa.get(BF16, (NCH, 256))
        b_qT, b_kT, b_kt, b_v = Buf("qT"), Buf("kT"), Buf("ktok"), Buf("vtok")
        sil = [wa.get(BF16, (512,)) for _ in range(2)]
        b_sil = [Buf() for _ in range(2)]
        mark0 = wa.cur
        vtok = wa.get(BF16, (NCH, 512))
        mark = wa.cur
        w1 = Builder.Alloc(self, mark0, OFF_WK + WK_SIZE - mark0)
        NQ3 = 3
        qs = [w1.get(F32, (8, 128)) for _ in range(NQ3)]
        sqb2 = [w1.get(F32, (8, 128)) for _ in range(NQ3)]
        tm1 = [w1.get(F32, (8, 64)) for _ in range(4)]
        qtok2 = [w1.get(BF16, (8, 128)) for _ in range(2)]
        ssq2 = [w1.get(F32, (8,)) for _ in range(NQ3)]
        b_qs = [Buf() for _ in range(NQ3)]
        b_sqb2 = [Buf() for _ in range(NQ3)]
        b_qtok2 = [Buf() for _ in range(2)]
        b_ssq2 = [Buf() for _ in range(NQ3)]
        b_tm1 = [Buf() for _ in range(4)]
        w2 = Builder.Alloc(self, mark, OFF_WK + WK_SIZE - mark)
        NST = 3
        st32 = w2.get(F32, (2, 256))
        stg = w2.get(F32, (2, 256))
        stbf = [w2.get(BF16, (2, 256)) for _ in range(NST)]
        b_st, b_stg = Buf("st32"), Buf("stg")
        b_stbf = [Buf() for _ in range(NST)]
        scm = [w2.get(BF16, (2, 128)) for _ in range(2)]
        b_scm = [Buf() for _ in range(2)]
        on = [w2.get(BF16, (2, 256)) for _ in range(2)]
        b_on = [Buf() for _ in range(2)]
        junk2 = w2.get(BF16, (256,))
        b_junk = Buf()
        ss2 = [w2.get(F32, (2,)) for _ in range(3)]
        rs2 = [w2.get(F32, (2,)) for _ in range(3)]
        b_ss2 = [Buf() for _ in range(3)]
        cnt = {"qk": 0, "sil": 0}
        nh8 = self.neghalf.to_broadcast([128, 8])
        nh2 = self.neghalf.to_broadcast([128, 2])

        def qk_s1(w, wbuf, g, which, hp, t):
            k = t % NQ3
            sqb, ssq = sqb2[k], ssq2[k]
            b_sqb, b_ssq = b_sqb2[k], b_ssq2[k]
            pbs = [self.rot("A_proj", [0, 1, 2, 3]) for _ in range(2)]
            for j in range(4):
                c = 4 * g + j
                pb = pbs[j // 2]
                o = self.ps[pb][:, (j % 2) * 256:(j % 2 + 1) * 256]
                for kc in range(8):
                    self.mm(o, self.hT[:, kc, c * 128:(c + 1) * 128], w[:, kc, :], kc == 0, kc == 7,
                            [self.b_hT, wbuf], [self.b_ps[pb]])
            for hh in range(2):
                self.cp("act", qs[k][:, 4 * hh:4 * hh + 4, :], self.ps[pbs[hh]][:].rearrange("p (a b) -> p a b", a=4),
                        [self.b_ps[pbs[hh]]], [b_qs[k]])
            for hh in range(2):
                self.act(sqb[:, 4 * hh:4 * hh + 4, :], self.ps[pbs[hh]][:].rearrange("p (a b) -> p a b", a=4), AF.Square,
                         [self.b_ps[pbs[hh]]], [b_sqb])
            P.op("dve", lambda e: e.tensor_reduce(ssq, sqb, AX.X, ALU.add), [b_sqb], [b_ssq])
            self.ts("dve", ssq, ssq, 1.0 / 128, EPS, ALU.mult, ALU.add, [b_ssq], [b_ssq])
            self.tt("pool", ssq, ssq, nh8, ALU.pow, [b_ssq, self.b_cst], [b_ssq])
            dcol = (0 if which == "q" else 4) + 2 * hp
            self.tt("pool", ssq.rearrange("p (j h) -> p j h", j=4), ssq.rearrange("p (j h) -> p j h", j=4),
                    self.dec[:, dcol:dcol + 2].unsqueeze(1).to_broadcast([128, 4, 2]), ALU.mult, [b_ssq, self.b_cst], [b_ssq])

        def qk_s2a(g, which, hp, t):
            k = t % NQ3
            sqb, tm, ssq = sqb2[k], tm1, ssq2[k]
            b_sqb, b_tm, b_ssq = b_sqb2[k], b_tm1, b_ssq2[k]
            g_ = gq if which == "q" else gk
            for i8 in range(8):
                self.act(sqb[:, i8, :], qs[k][:, i8, :], AF.Copy, [b_qs[k], b_ssq], [b_sqb], scale=ssq[:, i8:i8 + 1])
            self.tt("dve", sqb, sqb, g_.unsqueeze(1).to_broadcast([128, 8, 128]), ALU.mult, [b_sqb, b_pka], [b_sqb])
            x4 = sqb.rearrange("p (j h) d -> p j h d", j=4)
            x1 = x4[:, :, :, 0:64]
            x2 = x4[:, :, :, 64:128]
            cosb = self.cosT[:, 4 * g:4 * g + 4, :].unsqueeze(2).to_broadcast([128, 4, 2, 64])
            sinb = self.sinT[:, 4 * g:4 * g + 4, :].unsqueeze(2).to_broadcast([128, 4, 2, 64])
            t4 = [t.rearrange("p (j h) d -> p j h d", j=4) for t in tm]
            self.tt("dve", t4[0], x1, cosb, ALU.mult, [b_sqb, self.b_cst], [b_tm[0]])
            self.tt("dve", t4[1], x2, sinb, ALU.mult, [b_sqb, self.b_cst], [b_tm[1]])
            self.tt("pool", t4[2], x2, cosb, ALU.mult, [b_sqb, self.b_cst], [b_tm[2]])
            self.tt("dve", t4[3], x1, sinb, ALU.mult, [b_sqb, self.b_cst], [b_tm[3]])

        def qk_dst(g, which, t):
            k = t % 2
            if which == "q":
                return qtok2[k].rearrange("p (j h) d -> p j h d", j=4), b_qtok2[k]
            return ktok[:, 4 * g:4 * g + 4, :].rearrange("p j (h d) -> p j h d", h=2), b_kt

        def qk_s2b1(g, which, hp, t):
            tm, b_tm = tm1, b_tm1
            t4 = [t_.rearrange("p (j h) d -> p j h d", j=4) for t_ in tm]
            dst, bdst = qk_dst(g, which, t)
            self.tt("dve", dst[:, :, :, 0:64], t4[0], t4[1], ALU.subtract, [b_tm[0], b_tm[1]], [bdst])
            self.tt("pool", dst[:, :, :, 64:128], t4[2], t4[3], ALU.add, [b_tm[2], b_tm[3]], [bdst])

        def qk_s2b2(g, which, hp, t):
            dst, bdst = qk_dst(g, which, t)
            pt = self.rot("A_tr", [4, 5])
            psT = self.ps[pt][:].bitcast(BF16)
            for j in range(4):
                for h in range(2):
                    i = j * 2 + h
                    self.tr(psT[:, i * 128:(i + 1) * 128], dst[:, j, h, :], self.id_bf, [bdst, self.b_cst], [self.b_ps[pt]])
            dT, bdT = (qT, b_qT) if which == "q" else (kT, b_kT)
            self.cp("act", dT[:, :, 4 * g * 128:(4 * g + 4) * 128].rearrange("p h (j t) -> p j h t", j=4),
                    psT[:, 0:1024].rearrange("p (j h t) -> p j h t", j=4, h=2), [self.b_ps[pt]], [bdT])

        def qk_job(wq, wk, wbuf, hp):
            tasks = [("q", g, wq) for g in range(NG)] + [("k", g, wk) for g in range(NG)]
            NT = len(tasks)
            for it in range(NT + 3):
                if 0 <= it - 3 < NT:
                    wh, g, _ = tasks[it - 3]
                    qk_s2b1(g, wh, hp, it - 3)
                if 0 <= it - 2 < NT:
                    wh, g, _ = tasks[it - 2]
                    qk_s2a(g, wh, hp, it - 2)
                if it < NT:
                    wh, g, w_ = tasks[it]
                    qk_s1(w_, wbuf, g, wh, hp, it)
                if 0 <= it - 3 < NT:
                    wh, g, _ = tasks[it - 3]
                    qk_s2b2(g, wh, hp, it - 3)

        def v_chunk(w, wbuf, c):
            pb = self.rot("A_proj", [0, 1, 2, 3])
            csl = slice(c * 128, (c + 1) * 128)
            for kc in range(8):
                self.mm(self.ps[pb][:], self.hT[:, kc, csl], w[:, kc, :], kc == 0, kc == 7, [self.b_hT, wbuf], [self.b_ps[pb]])
            self.cp("act", vtok[:, c, :], self.ps[pb][:], [self.b_ps[pb]], [b_v])

        def recurrence(hp):
            self.memset("dve", st32, 0.0, [b_st])
            self.memset("dve", stbf[0], 0.0, [b_stbf[0]])
            pS = [0, 1]
            pK = [2, 3]
            pO = [4, 5, 6]
            pT = 7

            def stA(c):
                csl = slice(c * 128, (c + 1) * 128)
                bs, bk = pS[c % 2], pK[c % 2]
                for h in range(2):
                    self.mm(self.ps[bs][:, h * 128:(h + 1) * 128], kT[:, h, csl], qT[:, h, csl], True, True,
                            [b_kT, b_qT], [self.b_ps[bs]])
                for h in range(2):
                    self.mm(self.ps[bk][:, h * 256:(h + 1) * 256], ktok[:, c, h * 128:(h + 1) * 128],
                            vtok[:, c, h * 256:(h + 1) * 256], True, True, [b_kt, b_v], [self.b_ps[bk]])
                self.tt("dve", scm[c % 2], self.ps[bs][:, 0:256].rearrange("p (a b) -> p a b", a=2),
                        self.maskT.unsqueeze(1).to_broadcast([128, 2, 128]), ALU.mult, [self.b_ps[bs], self.b_cst], [b_scm[c % 2]])

            def stB(c):
                csl = slice(c * 128, (c + 1) * 128)
                po, bk = pO[c % 3], pK[c % 2]
                for h in range(2):
                    o = self.ps[po][:, h * 256:(h + 1) * 256]
                    self.mm(o, scm[c % 2][:, h, :], vtok[:, c, h * 256:(h + 1) * 256], True, False, [b_scm[c % 2], b_v], [self.b_ps[po]])
                    self.mm(o, qT[:, h, csl], stbf[c % NST][:, h, :], False, True, [b_qT, b_stbf[c % NST]], [self.b_ps[po]])
                if c + 1 < NCH:
                    for h in range(2):
                        g = G128[2 * hp + h]
                        self.tt("dve", stg[:, h, :], self.ps[bk][:, h * 256:(h + 1) * 256], st32[:, h, :], ALU.add,
                                [self.b_ps[bk], b_st], [b_stg])
                        self.ts("dve", st32[:, h, :], stg[:, h, :], g, None, ALU.mult, ALU.bypass, [b_stg], [b_st])
                        self.act(stbf[(c + 1) % NST][:, h, :], stg[:, h, :], AF.Copy, [b_stg], [b_stbf[(c + 1) % NST]], scale=g)

            def stC1(c):
                po, k = pO[c % 3], c % 3
                for h in range(2):
                    self.act(junk2, self.ps[po][:, h * 256:(h + 1) * 256], AF.Square, [self.b_ps[po]], [b_junk, b_ss2[k]],
                             accum_out=ss2[k][:, h:h + 1])
                self.ts("dve", rs2[k], ss2[k], 1.0 / 256, EPS, ALU.mult, ALU.add, [b_ss2[k]], [b_ss2[k]])
                self.tt("pool", rs2[k], rs2[k], nh2, ALU.pow, [b_ss2[k], self.b_cst], [b_ss2[k]])

            def stC2(c):
                po, k = pO[c % 3], c % 3
                for h in range(2):
                    hh = 2 * hp + h
                    self.stt(on[c % 2][:, h, :], self.ps[po][:, h * 256:(h + 1) * 256], rs2[k][:, h:h + 1],
                             go[:, hh * 256:(hh + 1) * 256], ALU.mult, ALU.mult, [self.b_ps[po], b_ss2[k], b_pka], [b_on[c % 2]])

            def stD(c):
                csl = slice(c * 128, (c + 1) * 128)
                psT = self.ps[pT][:].bitcast(BF16)
                for h in range(2):
                    for eh in range(2):
                        i = h * 2 + eh
                        self.tr(psT[:, i * 128:(i + 1) * 128], on[c % 2][:, h, eh * 128:(eh + 1) * 128], self.id_bf,
                                [b_on[c % 2], self.b_cst], [self.b_ps[pT]])
                self.cp("act", self.yaT[:, hp * 4:hp * 4 + 4, csl], psT[:, 0:512].rearrange("p (a b) -> p a b", a=4),
                        [self.b_ps[pT]], [self.b_ya])

            stA(0)
            for i in range(NCH + 2):
                if 0 <= i - 2 < NCH:
                    stC2(i - 2)
                    stD(i - 2)
                if i < NCH:
                    stB(i)
                if i + 1 < NCH:
                    stA(i + 1)
                if 0 <= i - 1 < NCH:
                    stC1(i - 1)
            P.barrier()

        jobs = []
        for hp in range(2):
            def fqk(views, buf, hp=hp):
                qk_job(views[0], views[1], buf, hp)

            def fv(views, buf, hp=hp):
                P.barrier()
                for c in range(NCH):
                    v_chunk(views[0], buf, c)
                recurrence(hp)
            jobs.append(([(self.win(l, C_RQ + hp * 256, 256), 8, 256), (self.win(l, C_RK + hp * 256, 256), 8, 256)], fqk))
            jobs.append(([(self.win(l, C_RV + hp * 512, 512), 8, 512)], fv))
        for fo in range(8):
            def fz(views, buf, fo=fo):
                w = views[0]
                for pc in range(NPC):
                    k = cnt["sil"] % 2
                    cnt["sil"] += 1
                    pb = self.rot("A_proj", [0, 1, 2, 3])
                    sl = slice(pc * 512, (pc + 1) * 512)
                    for kc in range(8):
                        self.mm(self.ps[pb][:], w[:, kc, :], self.hT[:, kc, sl], kc == 0, kc == 7, [buf, self.b_hT], [self.b_ps[pb]])
                    self.act(sil[k], self.ps[pb][:], AF.Silu, [self.b_ps[pb]], [b_sil[k]])
                    self.tt("dve", self.yaT[:, fo, sl], self.yaT[:, fo, sl], sil[k], ALU.mult, [self.b_ya, b_sil[k]], [self.b_ya])
            jobs.append(([(self.win(l, C_RZ + fo * 128, 128), 8, 128)], fz))
        self.run_jobs(jobs)

    def phaseB(self, l, s):
        P = self.P
        S, NCH, NPC = self.S, self.NCH, self.NPC
        wa = Builder.Alloc(self, OFF_WK, WK_SIZE)
        qT = wa.get(BF16, (4, S))
        kT = wa.get(BF16, (4, S))
        vtok = wa.get(BF16, (NCH, 512))
        b_qT, b_kT, b_v = Buf("sqT"), Buf("skT"), Buf("sv")
        mark = wa.cur
        w1 = Builder.Alloc(self, mark, OFF_WK + WK_SIZE - mark)
        sq = [w1.get(F32, (512,)) for _ in range(2)]
        lnv = [w1.get(F32, (512,)) for _ in range(2)]
        b_sq = [Buf() for _ in range(2)]
        b_ln = [Buf() for _ in range(2)]
        wa = Builder.Alloc(self, mark, OFF_WK + WK_SIZE - mark)
        NE_, NP_, NS_, NW_ = 3, 2, 3, 2
        ee = [wa.get(BF16, (1024,)) for _ in range(NE_)]
        sp = [wa.get(BF16, (1024,)) for _ in range(NP_)]
        SL = [wa.get(BF16, (1024,)) for _ in range(NS_)]
        wT = [wa.get(BF16, (1024,)) for _ in range(NW_)]
        b_ee = [Buf() for _ in range(NE_)]
        b_sp = [Buf() for _ in range(NP_)]
        b_SL = [Buf() for _ in range(NS_)]
        b_wT = [Buf() for _ in range(NW_)]
        we = [wa.get(BF16, (1024,)) for _ in range(1)]
        b_we = [Buf() for _ in range(1)]
        w3 = Builder.Alloc(self, mark, OFF_WK + WK_SIZE - mark)
        sil = [w3.get(BF16, (512,)) for _ in range(2)]
        b_sil = [Buf() for _ in range(2)]
        cnt = {"n": 0, "sil": 0}

        def qk_job(which, hp2):
            def f(views, buf):
                w = views[0]
                pbs = {}

                def s1(pc):
                    k = pc % 2
                    pb = self.rot("B_p", [0, 1, 2])
                    pbs[pc] = pb
                    sl = slice(pc * 512, (pc + 1) * 512)
                    for kc in range(8):
                        self.mm(self.ps[pb][:], w[:, kc, :], self.hT[:, kc, sl], kc == 0, kc == 7, [buf, self.b_hT], [self.b_ps[pb]])
                    self.act(sq[k], self.ps[pb][:], AF.Square, [self.b_ps[pb]], [b_sq[k]])

                def s2(pc):
                    k = pc % 2
                    pb = pbs[pc]
                    p2 = self.rot("B_s", [3, 4])
                    sl = slice(pc * 512, (pc + 1) * 512)
                    self.mm(self.ps[p2][:], self.onesb, sq[k], True, True, [self.b_cst, b_sq[k]], [self.b_ps[p2]])
                    self.act(lnv[k], self.ps[p2][:], AF.Ln, [self.b_ps[p2]], [b_ln[k]], scale=1.0 / 64, bias=EPS)
                    if which == "q":
                        self.act(lnv[k], lnv[k], AF.Exp, [b_ln[k]], [b_ln[k]], scale=-0.5)
                        gcol = self.pk[:, PK_SBQ:PK_SBQ + 1]
                        dst, bd = qT, b_qT
                    else:
                        self.act(lnv[k], lnv[k], AF.Exp, [b_ln[k]], [b_ln[k]], scale=-0.5, bias=math.log(0.125))
                        gcol = self.pk[:, PK_SBK:PK_SBK + 1]
                        dst, bd = kT, b_kT
                    self.stt(dst[:, hp2, sl], self.ps[pb][:], gcol, lnv[k], ALU.mult, ALU.mult,
                             [self.b_ps[pb], b_ln[k], self.b_cst], [bd])

                s1(0)
                for pc in range(NPC):
                    if pc + 1 < NPC:
                        s1(pc + 1)
                    s2(pc)
            return f

        def v_job(views, buf):
            w = views[0]
            for c in range(NCH):
                pb = self.rot("B_p", [0, 1])
                csl = slice(c * 128, (c + 1) * 128)
                for kc in range(8):
                    self.mm(self.ps[pb][:], self.hT[:, kc, csl], w[:, kc, :], kc == 0, kc == 7, [self.b_hT, buf], [self.b_ps[pb]])
                self.cp("act", vtok[:, c, :], self.ps[pb][:], [self.b_ps[pb]], [b_v])

        def attention():
            items = []
            gi = 0
            for hp in range(4):
                for qp in range(NPC):
                    kbs = list(range(4 * qp + 3, -1, -1))
                    for n_, kb in enumerate(kbs):
                        items.append(dict(hp=hp, qp=qp, kb=kb, first=(n_ == 0), last=(kb == 0), g=gi))
                    gi += 1
            n = len(items)
            D12, D23 = 1, 1

            def lo_of(it):
                o = it["kb"] - 4 * it["qp"]
                return 128 * o if o > 0 else 0

            def v2(t, lo):
                return t.rearrange("p (e c) -> p e c", e=2)[:, :, lo:512]

            def stage1(i):
                it = items[i]
                hp, qp, kb = it["hp"], it["qp"], it["kb"]
                xs = i % 2
                X = self.ps2[xs]
                bX = [self.b_ps[2 * xs], self.b_ps[2 * xs + 1]]
                diag = kb >= 4 * qp
                lo = lo_of(it)
                for e in range(2):
                    hb = 64 * e
                    o = X[:, e * 512 + lo:(e + 1) * 512]
                    self.mm(o, kT[hb:hb + 64, hp, kb * 128:(kb + 1) * 128],
                            qT[hb:hb + 64, hp, qp * 512 + lo:(qp + 1) * 512], True, not diag, [b_kT, b_qT], [bX[e]])
                    if diag:
                        self.mm(o, self.id_bf, self.negm_bf[:, kb - 4 * qp, lo:512], False, True, [self.b_cst], [bX[e]])
                E_, S_ = ee[i % NE_], sp[i % NP_]
                self.act(v2(E_, lo), v2(X[:], lo), AF.Exp, bX, [b_ee[i % NE_]])
                self.act(v2(S_, lo), v2(E_, lo), AF.Ln, [b_ee[i % NE_]], [b_sp[i % NP_]], bias=1.0)
                if not it["last"]:
                    So = SL[i % NS_]
                    if it["first"]:
                        self.cp("pool", v2(So, lo), v2(S_, lo), [b_sp[i % NP_]], [b_SL[i % NS_]])
                    else:
                        self.tt("pool", v2(So, lo), v2(SL[(i - 1) % NS_], lo), v2(S_, lo), ALU.add,
                                [b_SL[(i - 1) % NS_], b_sp[i % NP_]], [b_SL[i % NS_]])
                    if lo > 0:
                        self.memset("pool", So.rearrange("p (e c) -> p e c", e=2)[:, :, lo - 128:lo], 0.0, [b_SL[i % NS_]])

            def stage2(i):
                it = items[i]
                A = self.ps2[2]
                bA = [self.b_ps[4], self.b_ps[5]]
                lo = lo_of(it)
                S_ = sp[i % NP_]
                for e in range(2):
                    o = A[:, e * 512 + lo:(e + 1) * 512]
                    self.mm(o, self.trin_bf, S_[:, e * 512 + lo:(e + 1) * 512], True, it["first"], [self.b_cst, b_sp[i % NP_]], [bA[e]])
                    if not it["first"]:
                        self.mm(o, self.onesn_bf, SL[(i - 1) % NS_][:, e * 512 + lo:(e + 1) * 512], False, True,
                                [self.b_cst, b_SL[(i - 1) % NS_]], [bA[e]])
                self.act(v2(we[0], lo), v2(A[:], lo), AF.Exp, bA, [b_we[0]])
                self.tt("dve", v2(wT[i % NW_], lo), v2(we[0], lo), v2(ee[i % NE_], lo), ALU.mult,
                        [b_we[0], b_ee[i % NE_]], [b_wT[i % NW_]])

            def stage3(i):
                it = items[i]
                hp, qp, kb = it["hp"], it["qp"], it["kb"]
                ob = 6 + it["g"] % 2
                lo = lo_of(it)
                for e in range(2):
                    hb = 64 * e
                    h = 2 * hp + e
                    self.mm(self.ps[ob][hb:hb + 64, lo:512], vtok[:, kb, h * 64:(h + 1) * 64],
                            wT[i % NW_][:, e * 512 + lo:(e + 1) * 512], it["first"], it["last"],
                            [b_v, b_wT[i % NW_]], [self.b_ps[ob]])
                if it["last"]:
                    self.cp("dve", self.ybT[:, hp, qp * 512:(qp + 1) * 512], self.ps[ob][:, :], [self.b_ps[ob]], [self.b_yb])

            for step in range(n + D12 + D23):
                if step < n:
                    stage1(step)
                if 0 <= step - D12 < n:
                    stage2(step - D12)
                if 0 <= step - D12 - D23 < n:
                    stage3(step - D12 - D23)
            P.barrier()

        jobs = []
        for which, c0 in (("q", C_SQ), ("k", C_SK)):
            for hp2 in range(4):
                jobs.append(([(self.win(l, c0 + hp2 * 128, 128), 8, 128)], qk_job(which, hp2)))

        def v_and_att(views, buf):
            v_job(views, buf)
            P.barrier()
            attention()
        jobs.append(([(self.win(l, C_SV, 512), 8, 512)], v_and_att))
        for fc in range(4):
            def fz(views, buf, fc=fc):
                w = views[0]
                for pc in range(NPC):
                    k = cnt["sil"] % 2
                    cnt["sil"] += 1
                    pb = self.rot("B_p", [0, 1])
                    sl = slice(pc * 512, (pc + 1) * 512)
                    for kc in range(8):
                        self.mm(self.ps[pb][:], w[:, kc, :], self.hT[:, kc, sl], kc == 0, kc == 7, [buf, self.b_hT], [self.b_ps[pb]])
                    self.act(sil[k], self.ps[pb][:], AF.Silu, [self.b_ps[pb]], [b_sil[k]])
                    self.tt("dve", self.ybT[:, fc, sl], self.ybT[:, fc, sl], sil[k], ALU.mult, [self.b_yb, b_sil[k]], [self.b_yb])
            jobs.append(([(self.win(l, C_SZ + fc * 128, 128), 8, 128)], fz))
        self.run_jobs(jobs)

    def poly_exp(self, out, x, tmp, deg, r, b):
        self.ts("dve", out, x, 1.0 / math.factorial(deg), 1.0 / math.factorial(deg - 1), ALU.mult, ALU.add, r, [b])
        for k in range(deg - 2, -1, -1):
            self.tt("dve", out, out, x, ALU.mult, r + [b], [b])
            self.ts("dve", out, out, 1.0 / math.factorial(k), None, ALU.add, ALU.bypass, [b], [b])

    def phaseC(self, l, s):
        P = self.P
        S, NCH, NPC, NC8, LEV, PAD = self.S, self.NCH, self.NPC, self.NC8, self.LEV, self.PAD
        NB = S // 512
        ta = Builder.Alloc(self, OFF_YA, 48 * KB)
        tapsT = ta.get(BF16, (4, 8, 128))
        ET = ta.get(BF16, (4, 16, 128))
        CT = ta.get(BF16, (16, 2, 8, 32))
        kre = ta.get(F32, (16, 8))
        kim = ta.get(F32, (16, 8))
        nkim = ta.get(F32, (16, 8))
        wglu = ta.get(BF16, (4, 512))
        b_wglu = Buf("wglu")
        b_tab = Buf("tables")
        P.dma("pool", wglu, self.d_wglu[l].rearrange("(kc p) n -> p kc n", p=128), writes=[b_wglu])
        wa = Builder.Alloc(self, OFF_WK, WK_SIZE)
        uTd = wa.get(BF16, (4, S))
        b_u = Buf("uTd")
        mark = wa.cur
        tab_all = self.arena[:, OFF_YA:OFF_YA + TAB_BYTES]
        assert ta.cur - 4 * KB <= OFF_YA + TAB_BYTES
        if s != 0:
            P.dma("sp", tab_all, self.d_tabs, reads=[self.b_dtabs], writes=[b_tab])
        else:
            self.build_tables(l, mark, tapsT, ET, CT, kre, kim, nkim, b_tab)
            P.barrier()
            P.dma("sp", self.d_tabs, tab_all, reads=[b_tab], writes=[self.b_dtabs])
        self.run_ssm(l, s, mark, uTd, b_u, tapsT, ET, CT, kre, kim, nkim, b_tab, wglu, b_wglu)

    def build_tables(self, l, mark, tapsT, ET, CT, kre, kim, nkim, b_tab):
        P = self.P
        S, NCH, NPC, NC8, LEV, PAD = self.S, self.NCH, self.NPC, self.NC8, self.LEV, self.PAD
        ba = Builder.Alloc(self, mark, OFF_WK + WK_SIZE - mark)
        pkc = ba.get(F32, (4, 16, 16))
        b_pkc = Buf("pkc")
        P.dma("sp", pkc, self.d_pkc[l].rearrange("p (w a m) -> p w a m", w=4, a=16), writes=[b_pkc])
        sm = lambda: ba.get(F32, (16,))
        are0 = self.pk[:, PK_ARE:PK_ARE + 16]
        aim0 = self.pk[:, PK_AIM:PK_AIM + 16]
        ldt = self.pk[:, PK_LDT:PK_LDT + 16]
        bc = self.b_cst
        bS = Buf("ssm_small")
        x8, dtt, lam, th, mag1, c1, s1, abre, abim, nr, den, t0, t1_, cfr, cfi = [sm() for _ in range(15)]
        tfs, trs = sm(), sm()
        tis = ba.get(I32, (16,))
        self.ts("dve", x8, ldt, 0.125, None, ALU.mult, ALU.bypass, [bc], [bS])
        self.poly_exp(dtt, x8, t0, 10, [bS], bS)
        for _ in range(3):
            self.tt("dve", dtt, dtt, dtt, ALU.mult, [bS], [bS])
        self.tt("dve", lam, dtt, are0, ALU.mult, [bS, bc], [bS])
        self.tt("dve", th, dtt, aim0, ALU.mult, [bS, bc], [bS])
        self.poly_exp(mag1, lam, t0, 7, [bS], bS)
        self.emit_sin(s1, th, tfs, tis, trs, 0.0, [bS], [bS], bS)
        self.emit_sin(c1, th, tfs, tis, trs, math.pi / 2, [bS], [bS], bS)
        self.tt("dve", abre, mag1, c1, ALU.mult, [bS], [bS])
        self.tt("dve", abim, mag1, s1, ALU.mult, [bS], [bS])
        self.ts("dve", nr, abre, -1.0, None, ALU.add, ALU.bypass, [bS], [bS])
        self.tt("dve", den, are0, are0, ALU.mult, [bc], [bS])
        self.tt("dve", t0, aim0, aim0, ALU.mult, [bc], [bS])
        self.tt("dve", den, den, t0, ALU.add, [bS], [bS])
        P.op("dve", lambda e: e.reciprocal(den, den), [bS], [bS])
        self.tt("dve", t0, nr, are0, ALU.mult, [bS, bc], [bS])
        self.tt("dve", t1_, abim, aim0, ALU.mult, [bS, bc], [bS])
        self.tt("dve", cfr, t0, t1_, ALU.add, [bS], [bS])
        self.tt("dve", cfr, cfr, den, ALU.mult, [bS], [bS])
        self.tt("dve", t0, abim, are0, ALU.mult, [bS, bc], [bS])
        self.tt("dve", t1_, nr, aim0, ALU.mult, [bS, bc], [bS])
        self.tt("dve", cfi, t0, t1_, ALU.subtract, [bS], [bS])
        self.tt("dve", cfi, cfi, den, ALU.mult, [bS], [bS])
        if "Csm" in self.dbg:
            for nm, t_ in (("dtt", dtt), ("lam", lam), ("th", th), ("mag1", mag1), ("c1", c1), ("s1", s1), ("cfr", cfr), ("cfi", cfi), ("x8", x8)):
                self.dbg.add(nm)
                self.dump(nm, t_, (128, 16), F32, [bS])
        Bbr = ba.get(F32, (16, 16))
        Bbi = ba.get(F32, (16, 16))
        tb0 = ba.get(F32, (16, 16))
        bre, bim, cre, cim = pkc[:, 0], pkc[:, 1], pkc[:, 2], pkc[:, 3]
        cfrb = cfr.unsqueeze(2).to_broadcast([128, 16, 16])
        cfib = cfi.unsqueeze(2).to_broadcast([128, 16, 16])
        self.tt("dve", Bbr, bre, cfrb, ALU.mult, [b_pkc, bS], [bS])
        self.tt("dve", tb0, bim, cfib, ALU.mult, [b_pkc, bS], [bS])
        self.tt("dve", Bbr, Bbr, tb0, ALU.subtract, [bS], [bS])
        self.tt("dve", Bbi, bim, cfrb, ALU.mult, [b_pkc, bS], [bS])
        self.tt("dve", tb0, bre, cfib, ALU.mult, [b_pkc, bS], [bS])
        self.tt("dve", Bbi, Bbi, tb0, ALU.add, [bS], [bS])
        NE = 9
        areE = ba.get(F32, (16, NE))
        aimE = ba.get(F32, (16, NE))
        pw_a = ba.get(F32, (16, NE))
        pw_f = ba.get(F32, (16, NE))
        pw_r = ba.get(F32, (16, NE))
        pw_i = ba.get(I32, (16, NE))

        def powers(outre, outim, evs, ne):
            evb = evs.unsqueeze(1).to_broadcast([128, 16, ne])
            lamb = lam.unsqueeze(2).to_broadcast([128, 16, ne])
            thb = th.unsqueeze(2).to_broadcast([128, 16, ne])
            A_ = pw_a[:, :, 0:ne]
            self.tt("dve", A_, lamb, evb, ALU.mult, [bS, bc], [bS])
            self.act(outim, A_, AF.Exp, [bS], [bS])
            self.tt("dve", A_, thb, evb, ALU.mult, [bS, bc], [bS])
            self.emit_sin(outre, A_, pw_f[:, :, 0:ne], pw_i[:, :, 0:ne], pw_r[:, :, 0:ne], math.pi / 2, [bS], [bS], bS)
            self.tt("dve", outre, outre, outim, ALU.mult, [bS], [bS])
            self.emit_sin(pw_a[:, :, 0:ne], A_, pw_f[:, :, 0:ne], pw_i[:, :, 0:ne], pw_r[:, :, 0:ne], 0.0, [bS], [bS], bS)
            self.tt("dve", outim, outim, pw_a[:, :, 0:ne], ALU.mult, [bS], [bS])

        powers(areE, aimE, self.ev[:, 0:9], 9)
        powers(kre[:, :, 0:LEV], kim[:, :, 0:LEV], self.ev[:, 9:9 + LEV], LEV)
        self.ts("dve", nkim[:, :, 0:LEV], kim[:, :, 0:LEV], -1.0, None, ALU.mult, ALU.bypass, [bS], [bS, b_tab])
        tq = [ba.get(F32, (4, 9, 16)) for _ in range(4)]
        YA = ba.get(F32, (4, 9, 32))
        YB = ba.get(F32, (4, 9, 32))
        XBr = ba.get(F32, (4, 32))
        XBi = ba.get(F32, (4, 32))
        Zbd = ba.get(BF16, (4, 16, 32))
        diagD = ba.get(F32, (128,))
        bB = Buf("blk")
        for t_ in (YA, YB, XBr, XBi):
            self.memset("dve", t_, 0.0, [bB])
        self.memset("dve", Zbd, 0.0, [bB])
        YA5 = YA.rearrange("p a t (h m) -> p a t h m", h=2)
        YB5 = YB.rearrange("p a t (h m) -> p a t h m", h=2)
        XBr4 = XBr.rearrange("p a (h m) -> p a h m", h=2)
        XBi4 = XBi.rearrange("p a (h m) -> p a h m", h=2)
        Zbd6 = Zbd.rearrange("p a (e r) (h m) -> p a e r h m", r=2, h=2)
        for b in range(4):
            Ps = slice(4 * b, 4 * b + 4)
            creb = cre[:, Ps, :].unsqueeze(2).to_broadcast([128, 4, 9, 16])
            cimb = cim[:, Ps, :].unsqueeze(2).to_broadcast([128, 4, 9, 16])
            areb = areE[:, Ps, :].unsqueeze(3).to_broadcast([128, 4, 9, 16])
            aimb = aimE[:, Ps, :].unsqueeze(3).to_broadcast([128, 4, 9, 16])
            self.tt("dve", tq[0], creb, areb, ALU.mult, [b_pkc, bS], [bB])
            self.tt("dve", tq[1], cimb, aimb, ALU.mult, [b_pkc, bS], [bB])
            self.tt("dve", tq[2], creb, aimb, ALU.mult, [b_pkc, bS], [bB])
            self.tt("dve", tq[3], cimb, areb, ALU.mult, [b_pkc, bS], [bB])
            self.tt("dve", tq[0], tq[0], tq[1], ALU.subtract, [bB], [bB])
            self.stt(tq[2], tq[2], -1.0, tq[3], ALU.mult, ALU.subtract, [bB], [bB])
            for a in range(2):
                ps_ = slice(a * 64, (a + 1) * 64)
                self.cp("dve", YA5[ps_, :, :, a, :], tq[0][ps_], [bB], [bB])
                self.cp("dve", YB5[ps_, :, :, a, :], tq[2][ps_], [bB], [bB])
                self.cp("dve", XBr4[ps_, :, a, :], Bbr[ps_, Ps, :], [bS], [bB])
                self.cp("dve", XBi4[ps_, :, a, :], Bbi[ps_, Ps, :], [bS], [bB])
            for hf in range(2):
                self.memset("dve", self.ps[hf][:], 0.0, [self.b_ps[hf]])
            for pl in range(4):
                for tau in range(8):
                    hf, t4_ = tau // 4, tau % 4
                    o = self.ps[hf][32 * pl:32 * pl + 32, t4_ * 128 + 32 * pl:t4_ * 128 + 32 * pl + 32]
                    self.mm(o, XBr[:, pl, :], YA[:, pl, tau, :], True, False, [bB], [self.b_ps[hf]],
                            tile_position=(0, 32 * pl))
                    self.mm(o, XBi[:, pl, :], YB[:, pl, tau, :], False, True, [bB], [self.b_ps[hf]],
                            tile_position=(0, 32 * pl))
            self.ts("dve", diagD, self.id_f, self.pk[:, PK_D + b:PK_D + b + 1], None, ALU.mult, ALU.bypass, [bc], [bB])
            for hf in range(2):
                self.cp("act", tapsT[:, b, 4 * hf:4 * hf + 4, :], self.ps[hf][:].rearrange("p (t c) -> p t c", t=4),
                        [self.b_ps[hf]], [b_tab])
            self.tt("dve", tapsT[:, b, 0, :], self.ps[0][:, 0:128], diagD, ALU.add, [self.b_ps[0], bB], [b_tab])
            self.cp("dve", CT[:, Ps, 0, :, :], YA[:, :, 1:9, :], [bB], [b_tab])
            self.cp("dve", CT[:, Ps, 1, :, :], YB[:, :, 1:9, :], [bB], [b_tab])
            Bbrb = Bbr[:, Ps, :].unsqueeze(2).to_broadcast([128, 4, 8, 16])
            Bbib = Bbi[:, Ps, :].unsqueeze(2).to_broadcast([128, 4, 8, 16])
            ar8 = areE[:, Ps, 0:8].unsqueeze(3).to_broadcast([128, 4, 8, 16])
            ai8 = aimE[:, Ps, 0:8].unsqueeze(3).to_broadcast([128, 4, 8, 16])
            z = [tq[i][:, :, 0:8, :] for i in range(4)]
            self.tt("dve", z[0], Bbrb, ar8, ALU.mult, [bS], [bB])
            self.tt("dve", z[1], Bbib, ai8, ALU.mult, [bS], [bB])
            self.tt("dve", z[2], Bbib, ar8, ALU.mult, [bS], [bB])
            self.tt("dve", z[3], Bbrb, ai8, ALU.mult, [bS], [bB])
            self.tt("dve", z[0], z[0], z[1], ALU.subtract, [bB], [bB])
            self.tt("dve", z[2], z[2], z[3], ALU.add, [bB], [bB])
            for a in range(2):
                ps_ = slice(a * 64, (a + 1) * 64)
                self.cp("dve", Zbd6[ps_, :, :, 0, a, :], z[0][ps_], [bB], [bB])
                self.cp("dve", Zbd6[ps_, :, :, 1, a, :], z[2][ps_], [bB], [bB])
            for q in range(4):
                pbk = 2 + q % 2
                for j4 in range(4):
                    er = 4 * q + j4
                    for pl in range(4):
                        self.mm(self.ps[pbk][32 * pl:32 * pl + 32, j4 * 128:(j4 + 1) * 128], Zbd[:, pl, er, :], self.id_bf,
                                True, True, [bB, bc], [self.b_ps[pbk]], tile_position=(0, 32 * pl))
                self.cp("act", ET[:, b, 4 * q:4 * q + 4, :], self.ps[pbk][:].rearrange("p (t c) -> p t c", t=4),
                        [self.b_ps[pbk]], [b_tab])
        P.barrier()
        if "tabs" in self.dbg:
            self.dump("tapsT", tapsT, (128, 4, 8, 128), BF16, [b_tab])
            self.dump("ET", ET, (128, 4, 16, 128), BF16, [b_tab])
            self.dump("CT", CT, (128, 16, 2, 8, 32), BF16, [b_tab])
            self.dump("kre", kre, (128, 16, 8), F32, [b_tab])
            self.dump("kim", kim, (128, 16, 8), F32, [b_tab])

    def run_ssm(self, l, s, mark, uTd, b_u, tapsT, ET, CT, kre, kim, nkim, b_tab, wglu, b_wglu):
        P = self.P
        S, NCH, NPC, NC8, LEV, PAD = self.S, self.NCH, self.NPC, self.NC8, self.LEV, self.PAD
        NB = S // 512
        bc = self.b_cst
        wa2 = Builder.Alloc(self, mark, OFF_WK + WK_SIZE - mark)
        ygT = wa2.get(BF16, (4, S))
        b_yg = Buf("ygT")
        Xb = [[wa2.get(F32, (2, PAD + NC8)) for _ in range(2)] for _ in range(2)]
        b_X = [[Buf() for _ in range(2)] for _ in range(2)]
        stage = [wa2.get(F32, (2, PAD + NC8)) for _ in range(2)]
        b_stage = [Buf() for _ in range(2)]
        for ri in range(2):
            self.memset("dve", stage[ri], 0.0, [b_stage[ri]])
        Hbf = [wa2.get(BF16, (4, NC8)) for _ in range(2)]
        b_H = Buf("Hbf")
        gx = [wa2.get(F32, (512,)) for _ in range(2)]
        g2 = [wa2.get(F32, (512,)) for _ in range(1)]
        gs = [wa2.get(F32, (512,)) for _ in range(1)]
        b_gx = [Buf() for _ in range(2)]
        b_g2 = [Buf() for _ in range(2)]
        b_gs = [Buf() for _ in range(2)]
        silcz = wa2.get(BF16, (S,))
        b_scz = Buf("silcz")
        sgl = [wa2.get(BF16, (512,)) for _ in range(2)]
        b_sgl = [Buf() for _ in range(2)]
        ygl = [wa2.get(BF16, (512,)) for _ in range(2)]
        b_ygl = [Buf() for _ in range(2)]
        for pp in range(2):
            for ri in range(2):
                self.memset("dve", Xb[pp][ri], 0.0, [b_X[pp][ri]])
        cnt = {"g": 0, "glu": 0}

        def u_job(fc):
            def f(views, buf):
                w = views[0]
                for pc in range(NPC):
                    pb = self.rot("C_p", [0, 1, 2, 3])
                    sl = slice(pc * 512, (pc + 1) * 512)
                    for kc in range(8):
                        self.mm(self.ps[pb][:], w[:, kc, :], self.hT[:, kc, sl], kc == 0, kc == 7, [buf, self.b_hT], [self.b_ps[pb]])
                    dst = uTd[:, fc, :].rearrange("p (j c) -> p c j", j=8)[:, pc * 64:(pc + 1) * 64, :]
                    self.cp("act", dst, self.ps[pb][:].rearrange("p (c j) -> p c j", j=8), [self.b_ps[pb]], [b_u])
            return f

        units = [(b_, hf_) for b_ in range(4) for hf_ in range(2)]

        def taps(b):
            for q in range(NB):
                for tau in range(8):
                    lo = max(512 * q, tau * NC8)
                    hi = 512 * (q + 1)
                    if lo >= hi:
                        continue
                    self.mm(self.ps[q][:, lo - 512 * q:hi - 512 * q], tapsT[:, b, tau, :],
                            uTd[:, b, lo - tau * NC8:hi - tau * NC8], tau == 0, False, [b_tab, b_u], [self.b_ps[q]])

        def s_stage(u):
            b, hf = units[u]
            for plh in range(2):
                pl = 2 * hf + plh
                for ri in range(2):
                    bk = 4 + plh * 2 + ri
                    for j in range(8):
                        er = (7 - j) * 2 + ri
                        self.mm(self.ps[bk][:, 0:NC8], ET[32 * pl:32 * pl + 32, b, er, :],
                                uTd[32 * pl:32 * pl + 32, b, j * NC8:(j + 1) * NC8], j == 0, j == 7,
                                [b_tab, b_u], [self.b_ps[bk]], tile_position=(32 * pl, 0))
                    self.cp("act", stage[ri][:, plh, PAD + 1:PAD + NC8], self.ps[bk][:, 0:NC8 - 1],
                            [self.b_ps[bk]], [b_stage[ri]])

        def ks(u):
            b, hf = units[u]
            for k in range(LEV):
                sh = 1 << k
                if k == 0:
                    src, bs = stage, b_stage
                else:
                    src, bs = Xb[k % 2], b_X[k % 2]
                dst, bd = Xb[(k + 1) % 2], b_X[(k + 1) % 2]
                first, second = [], []
                for plh in range(2):
                    Pg = 4 * b + 2 * hf + plh
                    kr = kre[:, Pg, k:k + 1]
                    ki = kim[:, Pg, k:k + 1]
                    nki = nkim[:, Pg, k:k + 1]
                    A_ = src[0][:, plh, PAD:PAD + NC8]
                    As = src[0][:, plh, PAD - sh:PAD + NC8 - sh]
                    B_ = src[1][:, plh, PAD:PAD + NC8]
                    Bs = src[1][:, plh, PAD - sh:PAD + NC8 - sh]
                    dR = dst[0][:, plh, PAD:PAD + NC8]
                    dI = dst[1][:, plh, PAD:PAD + NC8]
                    first.append((dR, As, kr, A_, [bs[0], b_tab], [bd[0]]))
                    first.append((dI, Bs, kr, B_, [bs[1], b_tab], [bd[1]]))
                    second.append((dR, Bs, nki, dR, [bs[1], b_tab], [bd[0]]))
                    second.append((dI, As, ki, dI, [bs[0], b_tab], [bd[1]]))
                for (o_, i0, sc, i1, r_, w_) in first + second:
                    self.stt(o_, i0, sc, i1, ALU.mult, ALU.add, r_, w_)
                if k == 0 and u + 1 < len(units):
                    s_stage(u + 1)
            fin = Xb[LEV % 2]
            bf = b_X[LEV % 2]
            for ri in range(2):
                self.cp("act", Hbf[ri][:, 2 * hf:2 * hf + 2, :], fin[ri][:, :, PAD:PAD + NC8], [bf[ri]], [b_H])

        def cross_gelu(b):
            for i in range(8):
                q = (i * NC8) // 512
                off = i * NC8 - 512 * q
                for pl in range(4):
                    for ri in range(2):
                        lastmm = (i == min(7, (512 * (q + 1) - 1) // NC8)) and pl == 3 and ri == 1
                        self.mm(self.ps[q][32 * pl:32 * pl + 32, off:off + NC8], CT[:, 4 * b + pl, ri, i, :], Hbf[ri][:, pl, :],
                                False, lastmm, [b_tab, b_H], [self.b_ps[q]], tile_position=(0, 32 * pl), skip_group_check=True)
            for q in range(NB):
                k = cnt["g"] % 2
                cnt["g"] += 1
                self.cp("act", gx[k], self.ps[q][:], [self.b_ps[q]], [b_gx[k]])
                self.act(g2[0], self.ps[q][:], AF.Square, [self.b_ps[q]], [b_g2[0]])
                self.act(g2[0], g2[0], AF.Identity, [b_g2[0], bc], [b_g2[0]], scale=0.044715 * 1.5957691216057308, bias=1.5957691216057308)
                self.tt("dve", g2[0], g2[0], gx[k], ALU.mult, [b_g2[0], b_gx[k]], [b_g2[0]])
                self.act(gs[0], g2[0], AF.Sigmoid, [b_g2[0]], [b_gs[0]])
                self.tt("dve", ygT[:, b, 512 * q:512 * (q + 1)], gx[k], gs[0], ALU.mult, [b_gx[k], b_gs[0]], [b_yg])

        def run_units():
            s_stage(0)
            for u in range(len(units)):
                b, hf = units[u]
                if hf == 0:
                    taps(b)
                ks(u)
                if hf == 1:
                    cross_gelu(b)

        st_ = {"glu": (wglu, b_wglu)}

        def cz_job(fo):
            def f(views, buf):
                w = views[0]
                wg, bg = st_["glu"]
                for pc in range(NPC):
                    pb = self.rot("C_p", [0, 1, 2, 3])
                    sl = slice(pc * 512, (pc + 1) * 512)
                    for kc in range(8):
                        self.mm(self.ps[pb][:], w[:, kc, :], self.hT[:, kc, sl], kc == 0, kc == 7, [buf, self.b_hT], [self.b_ps[pb]])
                    self.act(silcz[:, sl], self.ps[pb][:], AF.Silu, [self.b_ps[pb]], [b_scz])
                nj = 512 // NC8 if NC8 <= 512 else 1
                for q in range(NB):
                    k = cnt["glu"] % 2
                    cnt["glu"] += 1
                    pb = self.rot("C_p", [0, 1, 2, 3])
                    sl = slice(512 * q, 512 * (q + 1))
                    for fc in range(4):
                        self.mm(self.ps[pb][:], wg[:, fc, fo * 128:(fo + 1) * 128], ygT[:, fc, sl], fc == 0, fc == 3,
                                [bg, b_yg], [self.b_ps[pb]])
                    self.act(sgl[k], self.ps[pb][:], AF.Sigmoid, [self.b_ps[pb], bc], [b_sgl[k]],
                             bias=self.pk[:, PK_BG + fo:PK_BG + fo + 1])
                    self.tt("dve", ygl[k], ygT[:, fo, sl], sgl[k], ALU.mult, [b_yg, b_sgl[k]], [b_ygl[k]])
                    j0 = (512 * q) // NC8
                    ov = self.ycT[:, fo, :].rearrange("p (c j) -> p j c", j=8)[:, j0:j0 + nj, :]
                    cv = silcz.rearrange("p (c j) -> p j c", j=8)[:, j0:j0 + nj, :]
                    self.tt("dve", ov, ygl[k].rearrange("p (j c) -> p j c", j=nj), cv, ALU.mult, [b_ygl[k], b_scz], [self.b_yc])
            return f

        jobs = []
        for fc in range(4):
            jobs.append(([(self.win(l, C_CU + fc * 128, 128), 8, 128)], u_job(fc)))

        def run_blocks(views, buf):
            run_units()
        jobs.append(([], run_blocks))
        for fo in range(4):
            jobs.append(([(self.win(l, C_CZ + fo * 128, 128), 8, 128)], cz_job(fo)))
        self.run_jobs(jobs)
        self.dump("uTd", uTd, (128, 4, S), BF16, [b_u])
        self.dump("ygT", ygT, (128, 4, S), BF16, [b_yg])


def _pack_params(inp):
    L = NL
    f = lambda a: np.asarray(a, dtype=np.float32)
    pk = np.zeros((L, 128, NPK), np.float32)
    pka = np.zeros((L, 128, NPA), np.float32)
    pkc = np.zeros((L, 128, NPC_), np.float32)

    def sP(a):
        return a.reshape(16, 2, 64).transpose(1, 2, 0).reshape(128, 16)

    def sPm(a):
        return a.reshape(16, 2, 64, 16).transpose(1, 2, 0, 3).reshape(128, 256)
    for l in range(L):
        pk[l, :, PK_G:PK_G + 8] = f(inp["norm_g"])[l].reshape(8, 128).T
        pk[l, :, PK_ARE:PK_ARE + 16] = sP(f(inp["ssm_a_re"])[l])
        pk[l, :, PK_AIM:PK_AIM + 16] = sP(f(inp["ssm_a_im"])[l])
        pk[l, :, PK_LDT:PK_LDT + 16] = sP(np.broadcast_to(f(inp["ssm_log_dt"])[l][:, None], (32, 64)))
        pk[l, :, PK_D:PK_D + 4] = f(inp["ssm_d"])[l].reshape(4, 128).T
        pk[l, :, PK_BG:PK_BG + 4] = f(inp["ssm_b_glu"])[l].reshape(4, 128).T
        pk[l, :, PK_SBQ] = np.tile(f(inp["sb_q_norm"])[l], 2)
        pk[l, :, PK_SBK] = np.tile(f(inp["sb_k_norm"])[l], 2)
        pka[l, :, PA_Q:PA_Q + 128] = f(inp["ret_q_norm"])[l][None, :]
        pka[l, :, PA_K:PA_K + 128] = f(inp["ret_k_norm"])[l][None, :]
        pka[l, :, PA_O:PA_O + 1024] = f(inp["ret_out_norm"])[l][None, :]
        pkc[l, :, 0:256] = sPm(f(inp["ssm_b_re"])[l])
        pkc[l, :, 256:512] = sPm(f(inp["ssm_b_im"])[l])
        pkc[l, :, 512:768] = sPm(f(inp["ssm_c_re"])[l].transpose(0, 2, 1))
        pkc[l, :, 768:1024] = sPm(f(inp["ssm_c_im"])[l].transpose(0, 2, 1))
    return pk, pka, pkc


_NC_CACHE = {}


def _get_nc(S, NSEQ):
    key = (S, NSEQ)
    if key not in _NC_CACHE:
        _NC_CACHE[key] = Builder(S, NSEQ).build()
    return _NC_CACHE[key]


def kernel(**inputs):
    x = np.ascontiguousarray(np.asarray(inputs["x"], dtype=np.float32))
    B, S, _ = x.shape
    ncores = 8
    NSEQ = B // ncores
    pk, pka, pkc = _pack_params(inputs)
    cst = make_consts()
    shared = {
        "w_in": np.ascontiguousarray(np.asarray(inputs["w_in"], np.float32)),
        "proj_a": np.ascontiguousarray(np.asarray(inputs["proj_a"], np.float32)),
        "proj_b": np.ascontiguousarray(np.asarray(inputs["proj_b"], np.float32)),
        "proj_c": np.ascontiguousarray(np.asarray(inputs["proj_c"], np.float32)),
        "w_out": np.ascontiguousarray(np.asarray(inputs["w_out"], np.float32)),
        "w_glu": np.ascontiguousarray(np.asarray(inputs["ssm_w_glu"], np.float32)),
        "pk": pk, "pka": pka, "pkc": pkc, "cst": cst,
    }
    nc = _get_nc(S, NSEQ)
    in_maps = []
    for c in range(ncores):
        m = dict(shared)
        m["x"] = x[c * NSEQ:(c + 1) * NSEQ]
        in_maps.append(m)
    res = run_bass_kernel_spmd(nc, in_maps, core_ids=list(range(ncores)))
    out = np.concatenate([np.asarray(r["out"], dtype=np.float32) for r in res.results], axis=0)
    return out
```

```python
import contextlib
import math
import numpy as np
import concourse.bass as bass
import concourse.mybir as mybir
from concourse.bass_utils import run_bass_kernel_spmd

F32 = mybir.dt.float32
BF16 = mybir.dt.bfloat16
I32 = mybir.dt.int32
U8 = mybir.dt.uint8
AF = mybir.ActivationFunctionType
ALU = mybir.AluOpType
AX = mybir.AxisListType

N_DMA_SEMS = 24
SAME_ENGINE_RAW_SYNC = True


class Buf:
    __slots__ = ("name", "w", "r")

    def __init__(self, name=""):
        self.name = name
        self.w = None
        self.r = []


class Op:
    __slots__ = ("eng", "fn", "deps", "dma", "sig", "sem", "val", "prewait", "asyn", "raw")

    def __init__(self, eng, fn, dma, asyn=False):
        self.eng = eng
        self.fn = fn
        self.dma = dma
        self.asyn = asyn
        self.raw = set()
        self.deps = set()
        self.sig = False
        self.sem = None
        self.val = None
        self.prewait = None


class Prog:
    ENGS = ("pe", "act", "dve", "pool", "sp")

    def __init__(self, nc):
        self.nc = nc
        self.ops = []
        self.last = {e: None for e in self.ENGS}
        self.pending_dma = []

    def op(self, eng, fn, reads=(), writes=(), dma=False, asyn=False):
        o = Op(eng, fn, dma, asyn)
        oid = len(self.ops)
        for b in reads:
            if b.w is not None:
                o.deps.add(b.w)
                o.raw.add(b.w)
        for b in writes:
            if b.w is not None:
                o.deps.add(b.w)
            o.deps.update(b.r)
        for b in reads:
            b.r.append(oid)
        for b in writes:
            b.w = oid
            b.r = []
        self.ops.append(o)
        if fn is not None:
            self.last[eng] = oid
            if dma:
                self.pending_dma.append(oid)
        return oid

    def dma(self, eng, out, in_, reads=(), writes=(), **kw):
        return self.op(eng, lambda e: e.dma_start(out=out, in_=in_, **kw), reads, writes, dma=True)

    def barrier(self):
        deps = set(x for x in self.last.values() if x is not None)
        deps.update(self.pending_dma)
        self.pending_dma = []
        for e in self.ENGS:
            o = Op(e, None, False)
            o.deps = set(deps)
            self.ops.append(o)

    def emit(self):
        nc = self.nc
        ops = self.ops

        def needs_wait(o, p, d):
            if p.fn is None:
                return False
            if p.eng == o.eng and not p.dma and not p.asyn:
                return SAME_ENGINE_RAW_SYNC and o.eng != "pe" and d in o.raw
            return True

        for o in ops:
            for d in o.deps:
                if needs_wait(o, ops[d], d):
                    ops[d].sig = True
        with contextlib.ExitStack() as st:
            esem = {e: st.enter_context(nc.semaphore("s_" + e)) for e in self.ENGS}
            dsem = [st.enter_context(nc.semaphore("d%d" % i)) for i in range(N_DMA_SEMS)]
            cnt = {e: 0 for e in self.ENGS}
            nd = 0
            for o in ops:
                if o.fn is None:
                    continue
                if o.dma:
                    s = nd % N_DMA_SEMS
                    k = nd // N_DMA_SEMS
                    o.sem = dsem[s]
                    o.val = 16 * (k + 1)
                    o.prewait = (dsem[s], 16 * k) if k > 0 else None
                    o.sig = True
                    nd += 1
                elif o.sig:
                    cnt[o.eng] += 1
                    o.sem = esem[o.eng]
                    o.val = cnt[o.eng]
            streams = {e: [] for e in self.ENGS}
            for o in ops:
                streams[o.eng].append(o)
            block = st.enter_context(nc.Block())

            def run(engname, e):
                waited = {}
                for o in streams[engname]:
                    best = {}
                    if o.prewait is not None:
                        best[o.prewait[0].num] = o.prewait
                    for d in o.deps:
                        p = ops[d]
                        if not needs_wait(o, p, d):
                            continue
                        cur = best.get(p.sem.num)
                        if cur is None or p.val > cur[1]:
                            best[p.sem.num] = (p.sem, p.val)
                    for sn in sorted(best):
                        s, v = best[sn]
                        if waited.get(sn, 0) >= v:
                            continue
                        e.wait_ge(s, v)
                        waited[sn] = v
                    if o.fn is None:
                        continue
                    ins = o.fn(e)
                    if o.sig:
                        ins.then_inc(o.sem, 16 if o.dma else 1)

            @block.tensor
            def _(e):
                run("pe", e)

            @block.scalar
            def _(e):
                run("act", e)

            @block.vector
            def _(e):
                run("dve", e)

            @block.gpsimd
            def _(e):
                run("pool", e)

            @block.sync
            def _(e):
                run("sp", e)


D = 1024
NKC = 8
NL = 2
EPS = 1e-6
C_RQ, C_RK, C_RV, C_RZ = 0, 512, 1024, 2048
C_SQ, C_SK, C_SV, C_SZ = 3072, 3584, 4096, 4608
C_CU, C_CZ = 5120, 5632
C_GA, C_GB, C_GC = 6144, 7168, 8192
IN_W = 9216
NEG = -30000.0
TWO_PI = 2.0 * math.pi
CW1 = 6.28125
CW2 = float(np.float32(TWO_PI - CW1))
CW3 = float(TWO_PI - CW1 - CW2)
PI_CL = 3.1415925

PK_G = 0
PK_ARE = 8
PK_AIM = 24
PK_LDT = 40
PK_D = 56
PK_BG = 60
PK_SBQ = 64
PK_SBK = 65
NPK = 66
PA_Q = 0
PA_K = 128
PA_O = 256
NPA = 1280
NPC_ = 1024

CS_ID = 0
CS_MASKT = 128
CS_TRIN = 256
CS_ONESN = 384
CS_ONESB = 512
CS_NEGM = 640
CS_DECQ = 640 + 2048
CS_DECK = CS_DECQ + 4
CS_POS = CS_DECK + 4
CS_INVF = CS_POS + 16
CS_EV0 = CS_INVF + 64
CS_EV1 = CS_EV0 + 9
CS_BLK = CS_EV1 + 8
NCST = CS_BLK + 128


def make_consts():
    c = np.zeros((128, NCST), np.float32)
    idx = np.arange(128)
    c[:, CS_ID:CS_ID + 128] = np.eye(128, dtype=np.float32)
    c[:, CS_MASKT:CS_MASKT + 128] = (idx[:, None] <= idx[None, :]).astype(np.float32)
    c[:, CS_TRIN:CS_TRIN + 128] = -(idx[:, None] >= idx[None, :]).astype(np.float32)
    c[:, CS_ONESN:CS_ONESN + 128] = -1.0
    blk = (idx[:, None] // 64 == idx[None, :] // 64).astype(np.float32)
    c[:, CS_ONESB:CS_ONESB + 128] = blk
    t = np.arange(512)
    for o in range(4):
        s = o * 128 + idx
        c[:, CS_NEGM + o * 512:CS_NEGM + (o + 1) * 512] = np.where(s[:, None] >= t[None, :], NEG, 0.0)
    lg = np.log1p(-np.exp2(-5.0 - np.arange(4, dtype=np.float64)))
    c[:, CS_DECQ:CS_DECQ + 4] = np.exp((idx[:, None] + 1) * lg[None, :])
    c[:, CS_DECK:CS_DECK + 4] = np.exp(-(idx[:, None] + 1) * lg[None, :]) * (128.0 ** -0.5)
    c[:, CS_POS:CS_POS + 16] = (np.arange(16)[None, :] * 128 + idx[:, None]).astype(np.float32)
    half = 64
    invf = (np.float32(10000.0) ** (-np.arange(half, dtype=np.float32) / np.float32(half))).astype(np.float32)
    c[:, CS_INVF:CS_INVF + 64] = invf[None, :]
    c[:, CS_EV0:CS_EV0 + 9] = np.arange(9, dtype=np.float32)[None, :]
    c[:, CS_EV1:CS_EV1 + 8] = (8.0 * 2.0 ** np.arange(8))[None, :]
    c[:, CS_BLK:CS_BLK + 128] = (idx[:, None] // 32 == idx[None, :] // 32).astype(np.float32)
    return c


G128 = [float(np.exp(128.0 * np.log1p(-np.exp2(-5.0 - h)))) for h in range(4)]

KB = 1024
OFF_HT = 0
OFF_YA = 32 * KB
OFF_YB = 64 * KB
OFF_YC = 80 * KB
OFF_CST = 96 * KB
OFF_WS = 112 * KB
OFF_WK = 136 * KB
TAB_BYTES = 42 * KB
ARENA = 207 * KB
WK_SIZE = ARENA - OFF_WK


def _dsz(dt):
    return {F32: 4, BF16: 2, I32: 4, U8: 1}[dt]


class Builder:
    def __init__(self, S, NSEQ=2, layers=(0, 1), dbg=()):
        self.S = S
        self.NSEQ = NSEQ
        self.layers = tuple(layers)
        self.NCH = S // 128
        self.NPC = S // 512
        self.NC8 = S // 8
        self.LEV = int(round(math.log2(self.NC8)))
        self.PAD = self.NC8 // 2
        self.dbg = set(dbg)
        self.dbg_out = {}

    def V(self, off, dt, shape):
        n = 1
        for s in shape:
            n *= s
        nb = n * _dsz(dt)
        assert off % 4 == 0 and off + nb <= ARENA, (off, nb)
        ap = self.arena[:, off:off + nb].bitcast(dt)
        if len(shape) == 2:
            ap = ap.rearrange("p (a b) -> p a b", a=shape[0])
        elif len(shape) == 3:
            ap = ap.rearrange("p (a b c) -> p a b c", a=shape[0], b=shape[1])
        elif len(shape) == 4:
            ap = ap.rearrange("p (a b c d) -> p a b c d", a=shape[0], b=shape[1], c=shape[2])
        return ap

    class Alloc:
        def __init__(self, b, off, size):
            self.b, self.off, self.end, self.cur = b, off, off + size, off

        def get(self, dt, shape):
            n = 1
            for s in shape:
                n *= s
            nb = (n * _dsz(dt) + 63) // 64 * 64
            assert self.cur + nb <= self.end, ("alloc overflow", self.cur + nb - self.end)
            v = self.b.V(self.cur, dt, shape)
            self.cur += nb
            return v

    def act(self, out, in_, func, r, w, **kw):
        self.P.op("act", lambda e: e.activation(out, in_, func, **kw), r, w, asyn=("accum_out" in kw))

    def tt(self, eng, out, a, b, op, r, w):
        self.P.op(eng, lambda e: e.tensor_tensor(out, a, b, op), r, w)

    def ts(self, eng, out, a, s1, s2, op0, op1, r, w):
        self.P.op(eng, lambda e: e.tensor_scalar(out, a, s1, s2, op0, op1), r, w)

    def stt(self, out, in0, scalar, in1, op0, op1, r, w, accum_out=None):
        if accum_out is None:
            self.P.op("dve", lambda e: e.scalar_tensor_tensor(out, in0, scalar, in1, op0, op1), r, w)
        else:
            self.P.op("dve", lambda e: e.scalar_tensor_tensor(out, in0, scalar, in1, op0, op1, accum_out=accum_out), r, w, asyn=True)

    def cp(self, eng, out, in_, r, w):
        if eng == "act":
            self.P.op("act", lambda e: e.activation(out, in_, AF.Copy), r, w)
        else:
            self.P.op(eng, lambda e: e.tensor_copy(out, in_), r, w)

    def mm(self, out, lhsT, rhs, start, stop, r, w, **kw):
        self.P.op("pe", lambda e: e.matmul(out, lhsT, rhs, start=start, stop=stop, **kw), r, w)

    def tr(self, out, in_, ident, r, w):
        self.P.op("pe", lambda e: e.transpose(out, in_, ident), r, w)

    def memset(self, eng, ap, val, w):
        self.P.op(eng, lambda e: e.memset(ap, val), (), w)

    def dump(self, name, ap, shape, dt, r):
        if name not in self.dbg:
            return
        t = self.nc.dram_tensor("dbg_" + name, list(shape), dt, kind="ExternalOutput").ap()
        self.dbg_out[name] = t
        self.P.barrier()
        self.P.dma("sp", t, ap, reads=r, writes=[self.b_out])
        self.P.barrier()

    def wslot(self):
        i = self.ws_i % 3
        self.ws_i += 1
        return OFF_WS + i * 8 * KB, self.b_ws[i]

    def wload(self, parts):
        if not parts:
            return [], Buf()
        off, buf = self.wslot()
        views = []
        cur = off
        for (dv, kc, n) in parts:
            v = self.V(cur, BF16, (kc, n))
            cur += kc * n * 2
            assert cur <= off + 8 * KB
            self.P.dma("pool", v, dv, writes=[buf])
            views.append(v)
        return views, buf

    def run_jobs(self, jobs, depth=2):
        loaded = {}
        n = len(jobs)
        for i in range(min(depth, n)):
            loaded[i] = self.wload(jobs[i][0])
        for i in range(n):
            if i + depth < n:
                loaded[i + depth] = self.wload(jobs[i + depth][0])
            views, buf = loaded.pop(i)
            jobs[i][1](views, buf)

    def win(self, l, c0, n):
        return self.d_win[l].rearrange("(kc p) n -> p kc n", p=128)[:, :, c0:c0 + n]

    def emit_sin(self, out, x, tmpf, tmpi, tmpr, shift, r, w, bt):
        self.ts("dve", tmpf, x, shift, 1.0 / TWO_PI, ALU.add, ALU.mult, r, [bt])
        self.cp("dve", tmpi, tmpf, [bt], [bt])
        self.cp("dve", tmpf, tmpi, [bt], [bt])
        self.ts("dve", tmpr, x, shift, None, ALU.add, ALU.bypass, r, [bt])
        self.stt(tmpr, tmpf, -CW1, tmpr, ALU.mult, ALU.add, [bt], [bt])
        self.stt(tmpr, tmpf, -CW2, tmpr, ALU.mult, ALU.add, [bt], [bt])
        self.stt(tmpr, tmpf, -CW3, tmpr, ALU.mult, ALU.add, [bt], [bt])
        self.ts("dve", tmpr, tmpr, PI_CL, -PI_CL, ALU.min, ALU.max, [bt], [bt])
        self.act(out, tmpr, AF.Sin, [bt], w)

    def build(self):
        S, NSEQ = self.S, self.NSEQ
        nc = bass.Bass("TRN2", target_bir_lowering=False)
        self.nc = nc
        dt = nc.dram_tensor
        self.d_x = dt("x", [NSEQ, S, D], F32, kind="ExternalInput").ap()
        self.d_win = dt("w_in", [NL, D, IN_W], F32, kind="ExternalInput").ap()
        self.d_pa = dt("proj_a", [NL, 1024, D], F32, kind="ExternalInput").ap()
        self.d_pb = dt("proj_b", [NL, 512, D], F32, kind="ExternalInput").ap()
        self.d_pc = dt("proj_c", [NL, 512, D], F32, kind="ExternalInput").ap()
        self.d_wout = dt("w_out", [NL, D, D], F32, kind="ExternalInput").ap()
        self.d_wglu = dt("w_glu", [NL, 512, 512], F32, kind="ExternalInput").ap()
        self.d_pk = dt("pk", [NL, 128, NPK], F32, kind="ExternalInput").ap()
        self.d_pka = dt("pka", [NL, 128, NPA], F32, kind="ExternalInput").ap()
        self.d_pkc = dt("pkc", [NL, 128, NPC_], F32, kind="ExternalInput").ap()
        self.d_cst = dt("cst", [128, NCST], F32, kind="ExternalInput").ap()
        self.d_xs = dt("xs", [NSEQ, S, D], F32, kind="Internal").ap()
        self.d_tabs = dt("tabs", [128, TAB_BYTES], U8, kind="Internal").ap()
        self.b_dtabs = Buf("dtabs")
        self.d_out = dt("out", [NSEQ, S, D], F32, kind="ExternalOutput").ap()

        with contextlib.ExitStack() as st:
            self.arena = st.enter_context(nc.sbuf_tensor("arena", [128, ARENA], U8))
            self.ps2 = [st.enter_context(nc.psum_tensor("ps%d" % i, [128, 1024], F32)) for i in range(4)]
            self.ps = [self.ps2[j // 2][:, (j % 2) * 512:(j % 2 + 1) * 512] for j in range(8)]
            self.b_ps = [Buf("ps%d" % i) for i in range(8)]
            self.P = Prog(nc)
            self.b_ws = [Buf("ws%d" % i) for i in range(3)]
            self.ws_i = 0
            self.b_out = Buf("out")
            self.b_hT = Buf("hT")
            self.b_ya = Buf("ya")
            self.b_yb = Buf("yb")
            self.b_yc = Buf("yc")
            self.b_cst = Buf("cst")
            self.b_xs = [Buf("xs%d" % s) for s in range(NSEQ)]
            self.hT = self.V(OFF_HT, BF16, (8, S))
            self.yaT = self.V(OFF_YA, BF16, (8, S))
            self.ybT = self.V(OFF_YB, BF16, (4, S))
            self.ycT = self.V(OFF_YC, BF16, (4, S))
            ca = Builder.Alloc(self, OFF_CST, 16 * KB)
            self.id_bf = ca.get(BF16, (128,))
            self.trin_bf = ca.get(BF16, (128,))
            self.onesn_bf = ca.get(BF16, (128,))
            self.negm_bf = ca.get(BF16, (4, 512))
            self.id_f = ca.get(F32, (128,))
            self.maskT = ca.get(F32, (128,))
            self.onesb = ca.get(F32, (128,))
            self.dec = ca.get(F32, (8,))
            self.ev = ca.get(F32, (17,))
            self.cosT = ca.get(F32, (16, 64))
            self.sinT = ca.get(F32, (16, 64))
            self.neghalf = ca.get(F32, (1,))
            self.epsc = ca.get(F32, (1,))
            self.ln8c = ca.get(F32, (1,))
            self.onec = ca.get(F32, (1,))
            self.gc1 = ca.get(F32, (1,))
            self.pk = ca.get(F32, (NPK,))

            self.init_consts()
            for l in self.layers:
                self.P.barrier()
                self.P.dma("sp", self.pk, self.d_pk[l], writes=[self.b_cst])
                self.P.barrier()
                for s in range(NSEQ):
                    self.layer(l, s)
            self.P.barrier()
            self.P.emit()
        return nc

    def init_consts(self):
        P = self.P
        wa = Builder.Alloc(self, OFF_WK, WK_SIZE)
        craw = wa.get(F32, (NCST,))
        b_raw = Buf("craw")
        b_t = Buf("ctmp")
        bc = self.b_cst
        P.dma("sp", craw, self.d_cst, writes=[b_raw])
        self.cp("dve", self.id_bf, craw[:, CS_ID:CS_ID + 128], [b_raw], [bc])
        self.cp("dve", self.trin_bf, craw[:, CS_TRIN:CS_TRIN + 128], [b_raw], [bc])
        self.cp("dve", self.onesn_bf, craw[:, CS_ONESN:CS_ONESN + 128], [b_raw], [bc])
        self.cp("dve", self.negm_bf, craw[:, CS_NEGM:CS_NEGM + 2048].rearrange("p (a b) -> p a b", a=4), [b_raw], [bc])
        self.cp("dve", self.id_f, craw[:, CS_ID:CS_ID + 128], [b_raw], [bc])
        self.cp("dve", self.maskT, craw[:, CS_MASKT:CS_MASKT + 128], [b_raw], [bc])
        self.cp("dve", self.onesb, craw[:, CS_ONESB:CS_ONESB + 128], [b_raw], [bc])
        self.cp("dve", self.dec, craw[:, CS_DECQ:CS_DECQ + 8], [b_raw], [bc])
        self.cp("dve", self.ev, craw[:, CS_EV0:CS_EV0 + 17], [b_raw], [bc])
        self.memset("dve", self.neghalf, -0.5, [bc])
        self.memset("dve", self.epsc, EPS, [bc])
        self.memset("dve", self.ln8c, math.log(0.125), [bc])
        self.memset("dve", self.onec, 1.0, [bc])
        self.memset("dve", self.gc1, 1.5957691216057308, [bc])
        ang = wa.get(F32, (16, 64))
        tf = wa.get(F32, (16, 64))
        ti = wa.get(I32, (16, 64))
        trr = wa.get(F32, (16, 64))
        pos = craw[:, CS_POS:CS_POS + 16]
        invf = craw[:, CS_INVF:CS_INVF + 64]
        self.tt("dve", ang, pos.unsqueeze(2).to_broadcast([128, 16, 64]),
                invf.unsqueeze(1).to_broadcast([128, 16, 64]), ALU.mult, [b_raw], [b_t])
        b_tt = Buf("sin_tmp")
        self.emit_sin(self.sinT, ang, tf, ti, trr, 0.0, [b_t], [bc], b_tt)
        self.emit_sin(self.cosT, ang, tf, ti, trr, math.pi / 2, [b_t], [bc], b_tt)
        P.barrier()

    def layer(self, l, s):
        first = (l == self.layers[0])
        last = (l == self.layers[-1])
        xsrc = self.d_x[s] if first else self.d_xs[s]
        xdst = self.d_out[s] if last else self.d_xs[s]
        self.phase0(l, s, xsrc)
        self.P.barrier()
        self.dump("hT", self.hT, (128, 8, self.S), BF16, [self.b_hT])
        if "only0" in self.dbg:
            return
        self.phaseC(l, s)
        self.P.barrier()
        self.dump("ycT", self.ycT, (128, 4, self.S), BF16, [self.b_yc])
        if "onlyC" in self.dbg:
            return
        self.phaseA(l, s)
        self.P.barrier()
        self.dump("yaT", self.yaT, (128, 8, self.S), BF16, [self.b_ya])
        self.phaseB(l, s)
        self.P.barrier()
        self.dump("ybT", self.ybT, (128, 4, self.S), BF16, [self.b_yb])
        self.phaseF(l, s, xsrc, xdst, last)
        self.P.barrier()

    def phase0(self, l, s, xsrc):
        P = self.P
        wa = Builder.Alloc(self, OFF_WK, WK_SIZE)
        NS0 = 6
        xin = [wa.get(F32, (D,)) for _ in range(NS0)]
        hb = [wa.get(BF16, (D,)) for _ in range(NS0)]
        junk = wa.get(BF16, (D,))
        ss = [wa.get(F32, (1,)) for _ in range(NS0)]
        ms = [wa.get(F32, (1,)) for _ in range(NS0)]
        rs = [wa.get(F32, (1,)) for _ in range(NS0)]
        b_x = [Buf() for _ in range(NS0)]
        b_hb = [Buf() for _ in range(NS0)]
        b_s = [Buf() for _ in range(NS0)]
        b_j = Buf()
        gT = self.pk[:, PK_G:PK_G + 8]
        def p0_s1(c):
            k = c % NS0
            P.dma("sp" if c % 2 == 0 else "pool", xin[k], xsrc[c * 128:(c + 1) * 128, :], reads=[self.b_xs[s]], writes=[b_x[k]])
            self.stt(junk, xin[k], 1.0, xin[k], ALU.mult, ALU.mult, [b_x[k]], [b_j, b_s[k]], accum_out=ss[k])
            self.ts("dve", ms[k], ss[k], 1.0 / D, EPS, ALU.mult, ALU.add, [b_s[k]], [b_s[k]])
            self.tt("pool", rs[k], ms[k], self.neghalf, ALU.pow, [b_s[k], self.b_cst], [b_s[k]])

        def p0_s2(c):
            k = c % NS0
            pb = 4 + (c % 4)
            psT = self.ps[pb][:].bitcast(BF16)
            self.act(hb[k], xin[k], AF.Copy, [b_x[k], b_s[k]], [b_hb[k]], scale=rs[k])
            for kc in range(8):
                self.tr(psT[:, kc * 128:(kc + 1) * 128], hb[k][:, kc * 128:(kc + 1) * 128], self.id_bf,
                        [b_hb[k], self.b_cst], [self.b_ps[pb]])
            self.tt("dve", self.hT[:, :, c * 128:(c + 1) * 128],
                    psT[:, 0:1024].rearrange("p (a b) -> p a b", a=8),
                    gT.unsqueeze(2).to_broadcast([128, 8, 128]), ALU.mult,
                    [self.b_ps[pb], self.b_cst], [self.b_hT])

        LA = 5
        for c in range(min(LA, self.NCH)):
            p0_s1(c)
        for c in range(self.NCH):
            if c + LA < self.NCH:
                p0_s1(c + LA)
            p0_s2(c)

    def phaseF(self, l, s, xsrc, xdst, to_out):
        P = self.P
        S = self.S
        wa = Builder.Alloc(self, OFF_WK, WK_SIZE)
        mT = wa.get(BF16, (8, S))
        b_m = Buf("mT")
        sg = [[wa.get(F32, (512,)) for _ in range(3)] for _ in range(2)]
        b_sg = [[Buf() for _ in range(3)] for _ in range(2)]
        mm_ = [wa.get(F32, (512,)) for _ in range(2)]
        b_mm = [Buf() for _ in range(2)]
        wo = wa.get(BF16, (8, D))
        b_wo = Buf("w_out")
        wov = self.d_wout[l].rearrange("(kc p) n -> p kc n", p=128)

        def load_wout():
            for hh in range(2):
                P.dma("pool", wo[:, :, hh * 512:(hh + 1) * 512], wov[:, :, hh * 512:(hh + 1) * 512], writes=[b_wo])
        pav = lambda l_, f: self.d_pa[l_].rearrange("(kc p) n -> p kc n", p=128)[:, :, f * 128:(f + 1) * 128]
        pbv = lambda l_, f: self.d_pb[l_].rearrange("(kc p) n -> p kc n", p=128)[:, :, f * 128:(f + 1) * 128]
        pcv = lambda l_, f: self.d_pc[l_].rearrange("(kc p) n -> p kc n", p=128)[:, :, f * 128:(f + 1) * 128]
        jobs = []
        state = {}
        cnt = [0]

        def gates_fn(fo):
            def f(views, buf):
                state["g"] = (views, buf)
                if fo == 1:
                    load_wout()
            return f

        def proj_fn(fo):
            def f(views, buf):
                gv, gb_ = state["g"]
                pa_, pb_, pc_ = views
                for pc in range(self.NPC):
                    k = cnt[0] % 2
                    cnt[0] += 1
                    sl = slice(pc * 512, (pc + 1) * 512)
                    gbank = [self.rot("F_g", [0, 1, 2, 3]) for _ in range(3)]
                    pbank = [self.rot("F_p", [4, 5, 6, 7]) for _ in range(3)]
                    for gi in range(3):
                        for kc in range(8):
                            self.mm(self.ps[gbank[gi]][:], gv[gi][:, kc, :], self.hT[:, kc, sl], kc == 0, kc == 7,
                                    [gb_, self.b_hT], [self.b_ps[gbank[gi]]])
                    for kc in range(8):
                        self.mm(self.ps[pbank[0]][:], pa_[:, kc, :], self.yaT[:, kc, sl], kc == 0, kc == 7,
                                [buf, self.b_ya], [self.b_ps[pbank[0]]])
                    for kc in range(4):
                        self.mm(self.ps[pbank[1]][:], pb_[:, kc, :], self.ybT[:, kc, sl], kc == 0, kc == 3,
                                [buf, self.b_yb], [self.b_ps[pbank[1]]])
                    for kc in range(4):
                        self.mm(self.ps[pbank[2]][:], pc_[:, kc, :], self.ycT[:, kc, sl], kc == 0, kc == 3,
                                [buf, self.b_yc], [self.b_ps[pbank[2]]])
                    for gi in range(3):
                        self.act(sg[k][gi], self.ps[gbank[gi]][:], AF.Sigmoid, [self.b_ps[gbank[gi]]], [b_sg[k][gi]])
                    self.tt("dve", mm_[k], sg[k][0], self.ps[pbank[0]][:], ALU.mult, [b_sg[k][0], self.b_ps[pbank[0]]], [b_mm[k]])
                    self.tt("dve", sg[k][1], sg[k][1], self.ps[pbank[1]][:], ALU.mult, [b_sg[k][1], self.b_ps[pbank[1]]], [b_sg[k][1]])
                    self.tt("dve", sg[k][2], sg[k][2], self.ps[pbank[2]][:], ALU.mult, [b_sg[k][2], self.b_ps[pbank[2]]], [b_sg[k][2]])
                    self.tt("dve", mm_[k], mm_[k], sg[k][1], ALU.add, [b_mm[k], b_sg[k][1]], [b_mm[k]])
                    self.tt("dve", mT[:, fo, sl], mm_[k], sg[k][2], ALU.add, [b_mm[k], b_sg[k][2]], [b_m])
            return f

        for fo in range(8):
            jobs.append(([(self.win(l, C_GA + fo * 128, 128), 8, 128), (self.win(l, C_GB + fo * 128, 128), 8, 128),
                          (self.win(l, C_GC + fo * 128, 128), 8, 128)], gates_fn(fo)))
            jobs.append(([(pav(l, fo), 8, 128), (pbv(l, fo), 4, 128), (pcv(l, fo), 4, 128)], proj_fn(fo)))
        self.run_jobs(jobs, depth=1)
        if "mT" in self.dbg:
            P.barrier()
            self.dump("mT", mT, (128, 8, S), BF16, [b_m])
        for e_ in ("pool", "dve"):
            P.op(e_, None, writes=[self.b_hT])
        NS2 = 4
        xin = [self.V(OFF_HT + i * 4 * KB, F32, (D,)) for i in range(NS2)]
        res = [self.V(OFF_HT + 16 * KB + i * 4 * KB, F32, (D,)) for i in range(NS2)]
        b_x = [Buf() for _ in range(NS2)]
        b_r = [Buf() for _ in range(NS2)]
        for c in range(self.NCH):
            k = c % NS2
            P.dma("pool", xin[k], xsrc[c * 128:(c + 1) * 128, :], reads=[self.b_xs[s]], writes=[b_x[k]])
            banks = [self.rot("F2", [0, 1, 2, 3, 4, 5, 6, 7]) for _ in range(2)]
            for hh in range(2):
                for kc in range(8):
                    self.mm(self.ps[banks[hh]][:], mT[:, kc, c * 128:(c + 1) * 128], wo[:, kc, hh * 512:(hh + 1) * 512],
                            kc == 0, kc == 7, [b_m, b_wo], [self.b_ps[banks[hh]]])
            for hh in range(2):
                self.tt("dve", res[k][:, hh * 512:(hh + 1) * 512], self.ps[banks[hh]][:], xin[k][:, hh * 512:(hh + 1) * 512],
                        ALU.add, [self.b_ps[banks[hh]], b_x[k]], [b_r[k]])
            wr = [Buf()]
            P.dma("sp" if c % 2 == 0 else "act", xdst[c * 128:(c + 1) * 128, :], res[k], reads=[b_r[k]], writes=wr)

    def rot(self, key, banks):
        d = self.__dict__.setdefault("_rot", {})
        i = d.get(key, 0)
        d[key] = i + 1
        return banks[i % len(banks)]

    def phaseA(self, l, s):
        P = self.P
        S, NCH, NPC = self.S, self.NCH, self.NPC
        NG = NCH // 4
        wa = Builder.Alloc(self, OFF_WK, WK_SIZE)
        pka = wa.get(F32, (NPA,))
        b_pka = Buf("pka")
        P.dma("sp", pka, self.d_pka[l], writes=[b_pka])
        gq = pka[:, PA_Q:PA_Q + 128]
        gk = pka[:, PA_K:PA_K + 128]
        go = pka[:, PA_O:PA_O + 1024]
        qT = wa.get(BF16, (2, S))
        kT = wa.get(BF16, (2, S))
        ktok = wa.get(BF16, (NCH, 256))
        b_qT, b_kT, b_kt, b_v = Buf("qT"), Buf("kT"), Buf("ktok"), Buf("vtok")
        sil = [wa.get(BF16, (512,)) for _ in range(2)]
        b_sil = [Buf() for _ in range(2)]
        mark0 = wa.cur
        vtok = wa.get(BF16, (NCH, 512))
        mark = wa.cur
        w1 = Builder.Alloc(self, mark0, OFF_WK + WK_SIZE - mark0)
        NQ3 = 3
        qs = [w1.get(F32, (8, 128)) for _ in range(NQ3)]
        sqb2 = [w1.get(F32, (8, 128)) for _ in range(NQ3)]
        tm1 = [w1.get(F32, (8, 64)) for _ in range(4)]
        qtok2 = [w1.get(BF16, (8, 128)) for _ in range(2)]
        ssq2 = [w1.get(F32, (8,)) for _ in range(NQ3)]
        b_qs = [Buf() for _ in range(NQ3)]
        b_sqb2 = [Buf() for _ in range(NQ3)]
        b_qtok2 = [Buf() for _ in range(2)]
        b_ssq2 = [Buf() for _ in range(NQ3)]
        b_tm1 = [Buf() for _ in range(4)]
        w2 = Builder.Alloc(self, mark, OFF_WK + WK_SIZE - mark)
        NST = 3
        st32 = w2.get(F32, (2, 256))
        stg = w2.get(F32, (2, 256))
        stbf = [w2.get(BF16, (2, 256)) for _ in range(NST)]
        b_st, b_stg = Buf("st32"), Buf("stg")
        b_stbf = [Buf() for _ in range(NST)]
        scm = [w2.get(BF16, (2, 128)) for _ in range(2)]
        b_scm = [Buf() for _ in range(2)]
        on = [w2.get(BF16, (2, 256)) for _ in range(2)]
        b_on = [Buf() for _ in range(2)]
        junk2 = w2.get(BF16, (256,))
        b_junk = Buf()
        ss2 = [w2.get(F32, (2,)) for _ in range(3)]
        rs2 = [w2.get(F32, (2,)) for _ in range(3)]
        b_ss2 = [Buf() for _ in range(3)]
        cnt = {"qk": 0, "sil": 0}
        nh8 = self.neghalf.to_broadcast([128, 8])
        nh2 = self.neghalf.to_broadcast([128, 2])

        def qk_s1(w, wbuf, g, which, hp, t):
            k = t % NQ3
            sqb, ssq = sqb2[k], ssq2[k]
            b_sqb, b_ssq = b_sqb2[k], b_ssq2[k]
            pbs = [self.rot("A_proj", [0, 1, 2, 3]) for _ in range(2)]
            for j in range(4):
                c = 4 * g + j
                pb = pbs[j // 2]
                o = self.ps[pb][:, (j % 2) * 256:(j % 2 + 1) * 256]
                for kc in range(8):
                    self.mm(o, self.hT[:, kc, c * 128:(c + 1) * 128], w[:, kc, :], kc == 0, kc == 7,
                            [self.b_hT, wbuf], [self.b_ps[pb]])
            for hh in range(2):
                self.cp("act", qs[k][:, 4 * hh:4 * hh + 4, :], self.ps[pbs[hh]][:].rearrange("p (a b) -> p a b", a=4),
                        [self.b_ps[pbs[hh]]], [b_qs[k]])
            for hh in range(2):
                self.act(sqb[:, 4 * hh:4 * hh + 4, :], self.ps[pbs[hh]][:].rearrange("p (a b) -> p a b", a=4), AF.Square,
                         [self.b_ps[pbs[hh]]], [b_sqb])
            P.op("dve", lambda e: e.tensor_reduce(ssq, sqb, AX.X, ALU.add), [b_sqb], [b_ssq])
            self.ts("dve", ssq, ssq, 1.0 / 128, EPS, ALU.mult, ALU.add, [b_ssq], [b_ssq])
            self.tt("pool", ssq, ssq, nh8, ALU.pow, [b_ssq, self.b_cst], [b_ssq])
            dcol = (0 if which == "q" else 4) + 2 * hp
            self.tt("pool", ssq.rearrange("p (j h) -> p j h", j=4), ssq.rearrange("p (j h) -> p j h", j=4),
                    self.dec[:, dcol:dcol + 2].unsqueeze(1).to_broadcast([128, 4, 2]), ALU.mult, [b_ssq, self.b_cst], [b_ssq])

        def qk_s2a(g, which, hp, t):
            k = t % NQ3
            sqb, tm, ssq = sqb2[k], tm1, ssq2[k]
            b_sqb, b_tm, b_ssq = b_sqb2[k], b_tm1, b_ssq2[k]
            g_ = gq if which == "q" else gk
            for i8 in range(8):
                self.act(sqb[:, i8, :], qs[k][:, i8, :], AF.Copy, [b_qs[k], b_ssq], [b_sqb], scale=ssq[:, i8:i8 + 1])
            self.tt("dve", sqb, sqb, g_.unsqueeze(1).to_broadcast([128, 8, 128]), ALU.mult, [b_sqb, b_pka], [b_sqb])
            x4 = sqb.rearrange("p (j h) d -> p j h d", j=4)
            x1 = x4[:, :, :, 0:64]
            x2 = x4[:, :, :, 64:128]
            cosb = self.cosT[:, 4 * g:4 * g + 4, :].unsqueeze(2).to_broadcast([128, 4, 2, 64])
            sinb = self.sinT[:, 4 * g:4 * g + 4, :].unsqueeze(2).to_broadcast([128, 4, 2, 64])
            t4 = [t.rearrange("p (j h) d -> p j h d", j=4) for t in tm]
            self.tt("dve", t4[0], x1, cosb, ALU.mult, [b_sqb, self.b_cst], [b_tm[0]])
            self.tt("dve", t4[1], x2, sinb, ALU.mult, [b_sqb, self.b_cst], [b_tm[1]])
            self.tt("pool", t4[2], x2, cosb, ALU.mult, [b_sqb, self.b_cst], [b_tm[2]])
            self.tt("dve", t4[3], x1, sinb, ALU.mult, [b_sqb, self.b_cst], [b_tm[3]])

        def qk_dst(g, which, t):
            k = t % 2
            if which == "q":
                return qtok2[k].rearrange("p (j h) d -> p j h d", j=4), b_qtok2[k]
            return ktok[:, 4 * g:4 * g + 4, :].rearrange("p j (h d) -> p j h d", h=2), b_kt

        def qk_s2b1(g, which, hp, t):
            tm, b_tm = tm1, b_tm1
            t4 = [t_.rearrange("p (j h) d -> p j h d", j=4) for t_ in tm]
            dst, bdst = qk_dst(g, which, t)
            self.tt("dve", dst[:, :, :, 0:64], t4[0], t4[1], ALU.subtract, [b_tm[0], b_tm[1]], [bdst])
            self.tt("pool", dst[:, :, :, 64:128], t4[2], t4[3], ALU.add, [b_tm[2], b_tm[3]], [bdst])

        def qk_s2b2(g, which, hp, t):
            dst, bdst = qk_dst(g, which, t)
            pt = self.rot("A_tr", [4, 5])
            psT = self.ps[pt][:].bitcast(BF16)
            for j in range(4):
                for h in range(2):
                    i = j * 2 + h
                    self.tr(psT[:, i * 128:(i + 1) * 128], dst[:, j, h, :], self.id_bf, [bdst, self.b_cst], [self.b_ps[pt]])
            dT, bdT = (qT, b_qT) if which == "q" else (kT, b_kT)
            self.cp("act", dT[:, :, 4 * g * 128:(4 * g + 4) * 128].rearrange("p h (j t) -> p j h t", j=4),
                    psT[:, 0:1024].rearrange("p (j h t) -> p j h t", j=4, h=2), [self.b_ps[pt]], [bdT])

        def qk_job(wq, wk, wbuf, hp):
            tasks = [("q", g, wq) for g in range(NG)] + [("k", g, wk) for g in range(NG)]
            NT = len(tasks)
            for it in range(NT + 3):
                if 0 <= it - 3 < NT:
                    wh, g, _ = tasks[it - 3]
                    qk_s2b1(g, wh, hp, it - 3)
                if 0 <= it - 2 < NT:
                    wh, g, _ = tasks[it - 2]
                    qk_s2a(g, wh, hp, it - 2)
                if it < NT:
                    wh, g, w_ = tasks[it]
                    qk_s1(w_, wbuf, g, wh, hp, it)
                if 0 <= it - 3 < NT:
                    wh, g, _ = tasks[it - 3]
                    qk_s2b2(g, wh, hp, it - 3)

        def v_chunk(w, wbuf, c):
            pb = self.rot("A_proj", [0, 1, 2, 3])
            csl = slice(c * 128, (c + 1) * 128)
            for kc in range(8):
                self.mm(self.ps[pb][:], self.hT[:, kc, csl], w[:, kc, :], kc == 0, kc == 7, [self.b_hT, wbuf], [self.b_ps[pb]])
            self.cp("act", vtok[:, c, :], self.ps[pb][:], [self.b_ps[pb]], [b_v])

        def recurrence(hp):
            self.memset("dve", st32, 0.0, [b_st])
            self.memset("dve", stbf[0], 0.0, [b_stbf[0]])
            pS = [0, 1]
            pK = [2, 3]
            pO = [4, 5, 6]
            pT = 7

            def stA(c):
                csl = slice(c * 128, (c + 1) * 128)
                bs, bk = pS[c % 2], pK[c % 2]
                for h in range(2):
                    self.mm(self.ps[bs][:, h * 128:(h + 1) * 128], kT[:, h, csl], qT[:, h, csl], True, True,
                            [b_kT, b_qT], [self.b_ps[bs]])
                for h in range(2):
                    self.mm(self.ps[bk][:, h * 256:(h + 1) * 256], ktok[:, c, h * 128:(h + 1) * 128],
                            vtok[:, c, h * 256:(h + 1) * 256], True, True, [b_kt, b_v], [self.b_ps[bk]])
                self.tt("dve", scm[c % 2], self.ps[bs][:, 0:256].rearrange("p (a b) -> p a b", a=2),
                        self.maskT.unsqueeze(1).to_broadcast([128, 2, 128]), ALU.mult, [self.b_ps[bs], self.b_cst], [b_scm[c % 2]])

            def stB(c):
                csl = slice(c * 128, (c + 1) * 128)
                po, bk = pO[c % 3], pK[c % 2]
                for h in range(2):
                    o = self.ps[po][:, h * 256:(h + 1) * 256]
                    self.mm(o, scm[c % 2][:, h, :], vtok[:, c, h * 256:(h + 1) * 256], True, False, [b_scm[c % 2], b_v], [self.b_ps[po]])
                    self.mm(o, qT[:, h, csl], stbf[c % NST][:, h, :], False, True, [b_qT, b_stbf[c % NST]], [self.b_ps[po]])
                if c + 1 < NCH:
                    for h in range(2):
                        self.tt("dve", stg[:, h, :], self.ps[bk][:, h * 256:(h + 1) * 256], st32[:, h, :], ALU.add,
                                [self.b_ps[bk], b_st], [b_stg])
                    for h in range(2):
                        self.ts("dve", stbf[(c + 1) % NST][:, h, :], stg[:, h, :], G128[2 * hp + h], None, ALU.mult, ALU.bypass,
                                [b_stg], [b_stbf[(c + 1) % NST]])
                    for h in range(2):
                        self.ts("dve", st32[:, h, :], stg[:, h, :], G128[2 * hp + h], None, ALU.mult, ALU.bypass, [b_stg], [b_st])

            def stC1(c):
                po, k = pO[c % 3], c % 3
                for h in range(2):
                    self.act(junk2, self.ps[po][:, h * 256:(h + 1) * 256], AF.Square, [self.b_ps[po]], [b_junk, b_ss2[k]],
                             accum_out=ss2[k][:, h:h + 1])
                self.ts("dve", rs2[k], ss2[k], 1.0 / 256, EPS, ALU.mult, ALU.add, [b_ss2[k]], [b_ss2[k]])
                self.tt("pool", rs2[k], rs2[k], nh2, ALU.pow, [b_ss2[k], self.b_cst], [b_ss2[k]])

            def stC2(c):
                po, k = pO[c % 3], c % 3
                for h in range(2):
                    hh = 2 * hp + h
                    self.stt(on[c % 2][:, h, :], self.ps[po][:, h * 256:(h + 1) * 256], rs2[k][:, h:h + 1],
                             go[:, hh * 256:(hh + 1) * 256], ALU.mult, ALU.mult, [self.b_ps[po], b_ss2[k], b_pka], [b_on[c % 2]])

            def stD(c):
                csl = slice(c * 128, (c + 1) * 128)
                psT = self.ps[pT][:].bitcast(BF16)
                for h in range(2):
                    for eh in range(2):
                        i = h * 2 + eh
                        self.tr(psT[:, i * 128:(i + 1) * 128], on[c % 2][:, h, eh * 128:(eh + 1) * 128], self.id_bf,
                                [b_on[c % 2], self.b_cst], [self.b_ps[pT]])
                self.cp("act", self.yaT[:, hp * 4:hp * 4 + 4, csl], psT[:, 0:512].rearrange("p (a b) -> p a b", a=4),
                        [self.b_ps[pT]], [self.b_ya])

            stA(0)
            for i in range(NCH + 2):
                if 0 <= i - 2 < NCH:
                    stC2(i - 2)
                    stD(i - 2)
                if i < NCH:
                    stB(i)
                if i + 1 < NCH:
                    stA(i + 1)
                if 0 <= i - 1 < NCH:
                    stC1(i - 1)
            P.barrier()

        jobs = []
        for hp in range(2):
            def fqk(views, buf, hp=hp):
                qk_job(views[0], views[1], buf, hp)

            def fv(views, buf, hp=hp):
                P.barrier()
                for c in range(NCH):
                    v_chunk(views[0], buf, c)
                recurrence(hp)
            jobs.append(([(self.win(l, C_RQ + hp * 256, 256), 8, 256), (self.win(l, C_RK + hp * 256, 256), 8, 256)], fqk))
            jobs.append(([(self.win(l, C_RV + hp * 512, 512), 8, 512)], fv))
        for fo in range(8):
            def fz(views, buf, fo=fo):
                w = views[0]
                for pc in range(NPC):
                    k = cnt["sil"] % 2
                    cnt["sil"] += 1
                    pb = self.rot("A_proj", [0, 1, 2, 3])
                    sl = slice(pc * 512, (pc + 1) * 512)
                    for kc in range(8):
                        self.mm(self.ps[pb][:], w[:, kc, :], self.hT[:, kc, sl], kc == 0, kc == 7, [buf, self.b_hT], [self.b_ps[pb]])
                    self.act(sil[k], self.ps[pb][:], AF.Silu, [self.b_ps[pb]], [b_sil[k]])
                    self.tt("dve", self.yaT[:, fo, sl], self.yaT[:, fo, sl], sil[k], ALU.mult, [self.b_ya, b_sil[k]], [self.b_ya])
            jobs.append(([(self.win(l, C_RZ + fo * 128, 128), 8, 128)], fz))
        self.run_jobs(jobs)

    def phaseB(self, l, s):
        P = self.P
        S, NCH, NPC = self.S, self.NCH, self.NPC
        wa = Builder.Alloc(self, OFF_WK, WK_SIZE)
        qT = wa.get(BF16, (4, S))
        kT = wa.get(BF16, (4, S))
        vtok = wa.get(BF16, (NCH, 512))
        b_qT, b_kT, b_v = Buf("sqT"), Buf("skT"), Buf("sv")
        mark = wa.cur
        w1 = Builder.Alloc(self, mark, OFF_WK + WK_SIZE - mark)
        sq = [w1.get(F32, (512,)) for _ in range(2)]
        lnv = [w1.get(F32, (512,)) for _ in range(2)]
        b_sq = [Buf() for _ in range(2)]
        b_ln = [Buf() for _ in range(2)]
        wa = Builder.Alloc(self, mark, OFF_WK + WK_SIZE - mark)
        NE_, NP_, NS_, NW_ = 3, 2, 3, 2
        ee = [wa.get(BF16, (1024,)) for _ in range(NE_)]
        sp = [wa.get(BF16, (1024,)) for _ in range(NP_)]
        SL = [wa.get(BF16, (1024,)) for _ in range(NS_)]
        wT = [wa.get(BF16, (1024,)) for _ in range(NW_)]
        b_ee = [Buf() for _ in range(NE_)]
        b_sp = [Buf() for _ in range(NP_)]
        b_SL = [Buf() for _ in range(NS_)]
        b_wT = [Buf() for _ in range(NW_)]
        we = [wa.get(BF16, (1024,)) for _ in range(1)]
        b_we = [Buf() for _ in range(1)]
        w3 = Builder.Alloc(self, mark, OFF_WK + WK_SIZE - mark)
        sil = [w3.get(BF16, (512,)) for _ in range(2)]
        b_sil = [Buf() for _ in range(2)]
        cnt = {"n": 0, "sil": 0}

        def qk_job(which, hp2):
            def f(views, buf):
                w = views[0]
                pbs = {}

                def s1(pc):
                    k = pc % 2
                    pb = self.rot("B_p", [0, 1, 2])
                    pbs[pc] = pb
                    sl = slice(pc * 512, (pc + 1) * 512)
                    for kc in range(8):
                        self.mm(self.ps[pb][:], w[:, kc, :], self.hT[:, kc, sl], kc == 0, kc == 7, [buf, self.b_hT], [self.b_ps[pb]])
                    self.act(sq[k], self.ps[pb][:], AF.Square, [self.b_ps[pb]], [b_sq[k]])

                def s2(pc):
                    k = pc % 2
                    pb = pbs[pc]
                    p2 = self.rot("B_s", [3, 4])
                    sl = slice(pc * 512, (pc + 1) * 512)
                    self.mm(self.ps[p2][:], self.onesb, sq[k], True, True, [self.b_cst, b_sq[k]], [self.b_ps[p2]])
                    self.act(lnv[k], self.ps[p2][:], AF.Ln, [self.b_ps[p2]], [b_ln[k]], scale=1.0 / 64, bias=EPS)
                    if which == "q":
                        self.act(lnv[k], lnv[k], AF.Exp, [b_ln[k]], [b_ln[k]], scale=-0.5)
                        gcol = self.pk[:, PK_SBQ:PK_SBQ + 1]
                        dst, bd = qT, b_qT
                    else:
                        self.act(lnv[k], lnv[k], AF.Exp, [b_ln[k]], [b_ln[k]], scale=-0.5, bias=math.log(0.125))
                        gcol = self.pk[:, PK_SBK:PK_SBK + 1]
                        dst, bd = kT, b_kT
                    self.stt(dst[:, hp2, sl], self.ps[pb][:], gcol, lnv[k], ALU.mult, ALU.mult,
                             [self.b_ps[pb], b_ln[k], self.b_cst], [bd])

                s1(0)
                for pc in range(NPC):
                    if pc + 1 < NPC:
                        s1(pc + 1)
                    s2(pc)
            return f

        def v_job(views, buf):
            w = views[0]
            for c in range(NCH):
                pb = self.rot("B_p", [0, 1])
                csl = slice(c * 128, (c + 1) * 128)
                for kc in range(8):
                    self.mm(self.ps[pb][:], self.hT[:, kc, csl], w[:, kc, :], kc == 0, kc == 7, [self.b_hT, buf], [self.b_ps[pb]])
                self.cp("act", vtok[:, c, :], self.ps[pb][:], [self.b_ps[pb]], [b_v])

        def attention():
            items = []
            gi = 0
            for hp in range(4):
                for qp in range(NPC):
                    kbs = list(range(4 * qp + 3, -1, -1))
                    for n_, kb in enumerate(kbs):
                        items.append(dict(hp=hp, qp=qp, kb=kb, first=(n_ == 0), last=(kb == 0), g=gi))
                    gi += 1
            n = len(items)
            D12, D23 = 1, 1

            def lo_of(it):
                o = it["kb"] - 4 * it["qp"]
                return 128 * o if o > 0 else 0

            def v2(t, lo):
                return t.rearrange("p (e c) -> p e c", e=2)[:, :, lo:512]

            def stage1(i):
                it = items[i]
                hp, qp, kb = it["hp"], it["qp"], it["kb"]
                xs = i % 2
                X = self.ps2[xs]
                bX = [self.b_ps[2 * xs], self.b_ps[2 * xs + 1]]
                diag = kb >= 4 * qp
                lo = lo_of(it)
                for e in range(2):
                    hb = 64 * e
                    o = X[:, e * 512 + lo:(e + 1) * 512]
                    self.mm(o, kT[hb:hb + 64, hp, kb * 128:(kb + 1) * 128],
                            qT[hb:hb + 64, hp, qp * 512 + lo:(qp + 1) * 512], True, not diag, [b_kT, b_qT], [bX[e]])
                    if diag:
                        self.mm(o, self.id_bf, self.negm_bf[:, kb - 4 * qp, lo:512], False, True, [self.b_cst], [bX[e]])
                E_, S_ = ee[i % NE_], sp[i % NP_]
                self.act(v2(E_, lo), v2(X[:], lo), AF.Exp, bX, [b_ee[i % NE_]])
                self.act(v2(S_, lo), v2(E_, lo), AF.Ln, [b_ee[i % NE_]], [b_sp[i % NP_]], bias=1.0)
                if not it["last"]:
                    So = SL[i % NS_]
                    if it["first"]:
                        self.cp("pool", v2(So, lo), v2(S_, lo), [b_sp[i % NP_]], [b_SL[i % NS_]])
                    else:
                        self.tt("pool", v2(So, lo), v2(SL[(i - 1) % NS_], lo), v2(S_, lo), ALU.add,
                                [b_SL[(i - 1) % NS_], b_sp[i % NP_]], [b_SL[i % NS_]])
                    if lo > 0:
                        self.memset("pool", So.rearrange("p (e c) -> p e c", e=2)[:, :, lo - 128:lo], 0.0, [b_SL[i % NS_]])

            def stage2(i):
                it = items[i]
                A = self.ps2[2]
                bA = [self.b_ps[4], self.b_ps[5]]
                lo = lo_of(it)
                S_ = sp[i % NP_]
                for e in range(2):
                    o = A[:, e * 512 + lo:(e + 1) * 512]
                    self.mm(o, self.trin_bf, S_[:, e * 512 + lo:(e + 1) * 512], True, it["first"], [self.b_cst, b_sp[i % NP_]], [bA[e]])
                    if not it["first"]:
                        self.mm(o, self.onesn_bf, SL[(i - 1) % NS_][:, e * 512 + lo:(e + 1) * 512], False, True,
                                [self.b_cst, b_SL[(i - 1) % NS_]], [bA[e]])
                self.act(v2(we[0], lo), v2(A[:], lo), AF.Exp, bA, [b_we[0]])
                self.tt("dve", v2(wT[i % NW_], lo), v2(we[0], lo), v2(ee[i % NE_], lo), ALU.mult,
                        [b_we[0], b_ee[i % NE_]], [b_wT[i % NW_]])

            def stage3(i):
                it = items[i]
                hp, qp, kb = it["hp"], it["qp"], it["kb"]
                ob = 6 + it["g"] % 2
                lo = lo_of(it)
                for e in range(2):
                    hb = 64 * e
                    h = 2 * hp + e
                    self.mm(self.ps[ob][hb:hb + 64, lo:512], vtok[:, kb, h * 64:(h + 1) * 64],
                            wT[i % NW_][:, e * 512 + lo:(e + 1) * 512], it["first"], it["last"],
                            [b_v, b_wT[i % NW_]], [self.b_ps[ob]])
                if it["last"]:
                    self.cp("dve", self.ybT[:, hp, qp * 512:(qp + 1) * 512], self.ps[ob][:, :], [self.b_ps[ob]], [self.b_yb])

            for step in range(n + D12 + D23):
                if step < n:
                    stage1(step)
                if 0 <= step - D12 < n:
                    stage2(step - D12)
                if 0 <= step - D12 - D23 < n:
                    stage3(step - D12 - D23)
            P.barrier()

        jobs = []
        for which, c0 in (("q", C_SQ), ("k", C_SK)):
            for hp2 in range(4):
                jobs.append(([(self.win(l, c0 + hp2 * 128, 128), 8, 128)], qk_job(which, hp2)))

        def v_and_att(views, buf):
            v_job(views, buf)
            P.barrier()
            attention()
        jobs.append(([(self.win(l, C_SV, 512), 8, 512)], v_and_att))
        for fc in range(4):
            def fz(views, buf, fc=fc):
                w = views[0]
                for pc in range(NPC):
                    k = cnt["sil"] % 2
                    cnt["sil"] += 1
                    pb = self.rot("B_p", [0, 1])
                    sl = slice(pc * 512, (pc + 1) * 512)
                    for kc in range(8):
                        self.mm(self.ps[pb][:], w[:, kc, :], self.hT[:, kc, sl], kc == 0, kc == 7, [buf, self.b_hT], [self.b_ps[pb]])
                    self.act(sil[k], self.ps[pb][:], AF.Silu, [self.b_ps[pb]], [b_sil[k]])
                    self.tt("dve", self.ybT[:, fc, sl], self.ybT[:, fc, sl], sil[k], ALU.mult, [self.b_yb, b_sil[k]], [self.b_yb])
            jobs.append(([(self.win(l, C_SZ + fc * 128, 128), 8, 128)], fz))
        self.run_jobs(jobs)

    def poly_exp(self, out, x, tmp, deg, r, b):
        self.ts("dve", out, x, 1.0 / math.factorial(deg), 1.0 / math.factorial(deg - 1), ALU.mult, ALU.add, r, [b])
        for k in range(deg - 2, -1, -1):
            self.tt("dve", out, out, x, ALU.mult, r + [b], [b])
            self.ts("dve", out, out, 1.0 / math.factorial(k), None, ALU.add, ALU.bypass, [b], [b])

    def phaseC(self, l, s):
        P = self.P
        S, NCH, NPC, NC8, LEV, PAD = self.S, self.NCH, self.NPC, self.NC8, self.LEV, self.PAD
        NB = S // 512
        ta = Builder.Alloc(self, OFF_YA, 48 * KB)
        tapsT = ta.get(BF16, (4, 8, 128))
        ET = ta.get(BF16, (4, 16, 128))
        CT = ta.get(BF16, (16, 2, 8, 32))
        kre = ta.get(F32, (16, 8))
        kim = ta.get(F32, (16, 8))
        nkim = ta.get(F32, (16, 8))
        wglu = ta.get(BF16, (4, 512))
        b_wglu = Buf("wglu")
        b_tab = Buf("tables")
        P.dma("pool", wglu, self.d_wglu[l].rearrange("(kc p) n -> p kc n", p=128), writes=[b_wglu])
        wa = Builder.Alloc(self, OFF_WK, WK_SIZE)
        uTd = wa.get(BF16, (4, S))
        b_u = Buf("uTd")
        mark = wa.cur
        tab_all = self.arena[:, OFF_YA:OFF_YA + TAB_BYTES]
        assert ta.cur - 4 * KB <= OFF_YA + TAB_BYTES
        if s != 0:
            P.dma("sp", tab_all, self.d_tabs, reads=[self.b_dtabs], writes=[b_tab])
        else:
            self.build_tables(l, mark, tapsT, ET, CT, kre, kim, nkim, b_tab)
            P.barrier()
            P.dma("sp", self.d_tabs, tab_all, reads=[b_tab], writes=[self.b_dtabs])
        self.run_ssm(l, s, mark, uTd, b_u, tapsT, ET, CT, kre, kim, nkim, b_tab, wglu, b_wglu)

    def build_tables(self, l, mark, tapsT, ET, CT, kre, kim, nkim, b_tab):
        P = self.P
        S, NCH, NPC, NC8, LEV, PAD = self.S, self.NCH, self.NPC, self.NC8, self.LEV, self.PAD
        ba = Builder.Alloc(self, mark, OFF_WK + WK_SIZE - mark)
        pkc = ba.get(F32, (4, 16, 16))
        b_pkc = Buf("pkc")
        P.dma("sp", pkc, self.d_pkc[l].rearrange("p (w a m) -> p w a m", w=4, a=16), writes=[b_pkc])
        sm = lambda: ba.get(F32, (16,))
        are0 = self.pk[:, PK_ARE:PK_ARE + 16]
        aim0 = self.pk[:, PK_AIM:PK_AIM + 16]
        ldt = self.pk[:, PK_LDT:PK_LDT + 16]
        bc = self.b_cst
        bS = Buf("ssm_small")
        x8, dtt, lam, th, mag1, c1, s1, abre, abim, nr, den, t0, t1_, cfr, cfi = [sm() for _ in range(15)]
        tfs, trs = sm(), sm()
        tis = ba.get(I32, (16,))
        self.ts("dve", x8, ldt, 0.125, None, ALU.mult, ALU.bypass, [bc], [bS])
        self.poly_exp(dtt, x8, t0, 10, [bS], bS)
        for _ in range(3):
            self.tt("dve", dtt, dtt, dtt, ALU.mult, [bS], [bS])
        self.tt("dve", lam, dtt, are0, ALU.mult, [bS, bc], [bS])
        self.tt("dve", th, dtt, aim0, ALU.mult, [bS, bc], [bS])
        self.poly_exp(mag1, lam, t0, 7, [bS], bS)
        self.emit_sin(s1, th, tfs, tis, trs, 0.0, [bS], [bS], bS)
        self.emit_sin(c1, th, tfs, tis, trs, math.pi / 2, [bS], [bS], bS)
        self.tt("dve", abre, mag1, c1, ALU.mult, [bS], [bS])
        self.tt("dve", abim, mag1, s1, ALU.mult, [bS], [bS])
        self.ts("dve", nr, abre, -1.0, None, ALU.add, ALU.bypass, [bS], [bS])
        self.tt("dve", den, are0, are0, ALU.mult, [bc], [bS])
        self.tt("dve", t0, aim0, aim0, ALU.mult, [bc], [bS])
        self.tt("dve", den, den, t0, ALU.add, [bS], [bS])
        P.op("dve", lambda e: e.reciprocal(den, den), [bS], [bS])
        self.tt("dve", t0, nr, are0, ALU.mult, [bS, bc], [bS])
        self.tt("dve", t1_, abim, aim0, ALU.mult, [bS, bc], [bS])
        self.tt("dve", cfr, t0, t1_, ALU.add, [bS], [bS])
        self.tt("dve", cfr, cfr, den, ALU.mult, [bS], [bS])
        self.tt("dve", t0, abim, are0, ALU.mult, [bS, bc], [bS])
        self.tt("dve", t1_, nr, aim0, ALU.mult, [bS, bc], [bS])
        self.tt("dve", cfi, t0, t1_, ALU.subtract, [bS], [bS])
        self.tt("dve", cfi, cfi, den, ALU.mult, [bS], [bS])
        if "Csm" in self.dbg:
            for nm, t_ in (("dtt", dtt), ("lam", lam), ("th", th), ("mag1", mag1), ("c1", c1), ("s1", s1), ("cfr", cfr), ("cfi", cfi), ("x8", x8)):
                self.dbg.add(nm)
                self.dump(nm, t_, (128, 16), F32, [bS])
        Bbr = ba.get(F32, (16, 16))
        Bbi = ba.get(F32, (16, 16))
        tb0 = ba.get(F32, (16, 16))
        bre, bim, cre, cim = pkc[:, 0], pkc[:, 1], pkc[:, 2], pkc[:, 3]
        cfrb = cfr.unsqueeze(2).to_broadcast([128, 16, 16])
        cfib = cfi.unsqueeze(2).to_broadcast([128, 16, 16])
        self.tt("dve", Bbr, bre, cfrb, ALU.mult, [b_pkc, bS], [bS])
        self.tt("dve", tb0, bim, cfib, ALU.mult, [b_pkc, bS], [bS])
        self.tt("dve", Bbr, Bbr, tb0, ALU.subtract, [bS], [bS])
        self.tt("dve", Bbi, bim, cfrb, ALU.mult, [b_pkc, bS], [bS])
        self.tt("dve", tb0, bre, cfib, ALU.mult, [b_pkc, bS], [bS])
        self.tt("dve", Bbi, Bbi, tb0, ALU.add, [bS], [bS])
        NE = 9
        areE = ba.get(F32, (16, NE))
        aimE = ba.get(F32, (16, NE))
        pw_a = ba.get(F32, (16, NE))
        pw_f = ba.get(F32, (16, NE))
        pw_r = ba.get(F32, (16, NE))
        pw_i = ba.get(I32, (16, NE))

        def powers(outre, outim, evs, ne):
            evb = evs.unsqueeze(1).to_broadcast([128, 16, ne])
            lamb = lam.unsqueeze(2).to_broadcast([128, 16, ne])
            thb = th.unsqueeze(2).to_broadcast([128, 16, ne])
            A_ = pw_a[:, :, 0:ne]
            self.tt("dve", A_, lamb, evb, ALU.mult, [bS, bc], [bS])
            self.act(outim, A_, AF.Exp, [bS], [bS])
            self.tt("dve", A_, thb, evb, ALU.mult, [bS, bc], [bS])
            self.emit_sin(outre, A_, pw_f[:, :, 0:ne], pw_i[:, :, 0:ne], pw_r[:, :, 0:ne], math.pi / 2, [bS], [bS], bS)
            self.tt("dve", outre, outre, outim, ALU.mult, [bS], [bS])
            self.emit_sin(pw_a[:, :, 0:ne], A_, pw_f[:, :, 0:ne], pw_i[:, :, 0:ne], pw_r[:, :, 0:ne], 0.0, [bS], [bS], bS)
            self.tt("dve", outim, outim, pw_a[:, :, 0:ne], ALU.mult, [bS], [bS])

        powers(areE, aimE, self.ev[:, 0:9], 9)
        powers(kre[:, :, 0:LEV], kim[:, :, 0:LEV], self.ev[:, 9:9 + LEV], LEV)
        self.ts("dve", nkim[:, :, 0:LEV], kim[:, :, 0:LEV], -1.0, None, ALU.mult, ALU.bypass, [bS], [bS, b_tab])
        tq = [ba.get(F32, (4, 9, 16)) for _ in range(4)]
        YA = ba.get(F32, (4, 9, 32))
        YB = ba.get(F32, (4, 9, 32))
        XBr = ba.get(F32, (4, 32))
        XBi = ba.get(F32, (4, 32))
        Zbd = ba.get(BF16, (4, 16, 32))
        diagD = ba.get(F32, (128,))
        bB = Buf("blk")
        for t_ in (YA, YB, XBr, XBi):
            self.memset("dve", t_, 0.0, [bB])
        self.memset("dve", Zbd, 0.0, [bB])
        YA5 = YA.rearrange("p a t (h m) -> p a t h m", h=2)
        YB5 = YB.rearrange("p a t (h m) -> p a t h m", h=2)
        XBr4 = XBr.rearrange("p a (h m) -> p a h m", h=2)
        XBi4 = XBi.rearrange("p a (h m) -> p a h m", h=2)
        Zbd6 = Zbd.rearrange("p a (e r) (h m) -> p a e r h m", r=2, h=2)
        for b in range(4):
            Ps = slice(4 * b, 4 * b + 4)
            creb = cre[:, Ps, :].unsqueeze(2).to_broadcast([128, 4, 9, 16])
            cimb = cim[:, Ps, :].unsqueeze(2).to_broadcast([128, 4, 9, 16])
            areb = areE[:, Ps, :].unsqueeze(3).to_broadcast([128, 4, 9, 16])
            aimb = aimE[:, Ps, :].unsqueeze(3).to_broadcast([128, 4, 9, 16])
            self.tt("dve", tq[0], creb, areb, ALU.mult, [b_pkc, bS], [bB])
            self.tt("dve", tq[1], cimb, aimb, ALU.mult, [b_pkc, bS], [bB])
            self.tt("dve", tq[2], creb, aimb, ALU.mult, [b_pkc, bS], [bB])
            self.tt("dve", tq[3], cimb, areb, ALU.mult, [b_pkc, bS], [bB])
            self.tt("dve", tq[0], tq[0], tq[1], ALU.subtract, [bB], [bB])
            self.stt(tq[2], tq[2], -1.0, tq[3], ALU.mult, ALU.subtract, [bB], [bB])
            for a in range(2):
                ps_ = slice(a * 64, (a + 1) * 64)
                self.cp("dve", YA5[ps_, :, :, a, :], tq[0][ps_], [bB], [bB])
                self.cp("dve", YB5[ps_, :, :, a, :], tq[2][ps_], [bB], [bB])
                self.cp("dve", XBr4[ps_, :, a, :], Bbr[ps_, Ps, :], [bS], [bB])
                self.cp("dve", XBi4[ps_, :, a, :], Bbi[ps_, Ps, :], [bS], [bB])
            for hf in range(2):
                self.memset("dve", self.ps[hf][:], 0.0, [self.b_ps[hf]])
            for pl in range(4):
                for tau in range(8):
                    hf, t4_ = tau // 4, tau % 4
                    o = self.ps[hf][32 * pl:32 * pl + 32, t4_ * 128 + 32 * pl:t4_ * 128 + 32 * pl + 32]
                    self.mm(o, XBr[:, pl, :], YA[:, pl, tau, :], True, False, [bB], [self.b_ps[hf]],
                            tile_position=(0, 32 * pl))
                    self.mm(o, XBi[:, pl, :], YB[:, pl, tau, :], False, True, [bB], [self.b_ps[hf]],
                            tile_position=(0, 32 * pl))
            self.ts("dve", diagD, self.id_f, self.pk[:, PK_D + b:PK_D + b + 1], None, ALU.mult, ALU.bypass, [bc], [bB])
            for hf in range(2):
                self.cp("act", tapsT[:, b, 4 * hf:4 * hf + 4, :], self.ps[hf][:].rearrange("p (t c) -> p t c", t=4),
                        [self.b_ps[hf]], [b_tab])
            self.tt("dve", tapsT[:, b, 0, :], self.ps[0][:, 0:128], diagD, ALU.add, [self.b_ps[0], bB], [b_tab])
            self.cp("dve", CT[:, Ps, 0, :, :], YA[:, :, 1:9, :], [bB], [b_tab])
            self.cp("dve", CT[:, Ps, 1, :, :], YB[:, :, 1:9, :], [bB], [b_tab])
            Bbrb = Bbr[:, Ps, :].unsqueeze(2).to_broadcast([128, 4, 8, 16])
            Bbib = Bbi[:, Ps, :].unsqueeze(2).to_broadcast([128, 4, 8, 16])
            ar8 = areE[:, Ps, 0:8].unsqueeze(3).to_broadcast([128, 4, 8, 16])
            ai8 = aimE[:, Ps, 0:8].unsqueeze(3).to_broadcast([128, 4, 8, 16])
            z = [tq[i][:, :, 0:8, :] for i in range(4)]
            self.tt("dve", z[0], Bbrb, ar8, ALU.mult, [bS], [bB])
            self.tt("dve", z[1], Bbib, ai8, ALU.mult, [bS], [bB])
            self.tt("dve", z[2], Bbib, ar8, ALU.mult, [bS], [bB])
            self.tt("dve", z[3], Bbrb, ai8, ALU.mult, [bS], [bB])
            self.tt("dve", z[0], z[0], z[1], ALU.subtract, [bB], [bB])
            self.tt("dve", z[2], z[2], z[3], ALU.add, [bB], [bB])
            for a in range(2):
                ps_ = slice(a * 64, (a + 1) * 64)
                self.cp("dve", Zbd6[ps_, :, :, 0, a, :], z[0][ps_], [bB], [bB])
                self.cp("dve", Zbd6[ps_, :, :, 1, a, :], z[2][ps_], [bB], [bB])
            for q in range(4):
                pbk = 2 + q % 2
                for j4 in range(4):
                    er = 4 * q + j4
                    for pl in range(4):
                        self.mm(self.ps[pbk][32 * pl:32 * pl + 32, j4 * 128:(j4 + 1) * 128], Zbd[:, pl, er, :], self.id_bf,
                                True, True, [bB, bc], [self.b_ps[pbk]], tile_position=(0, 32 * pl))
                self.cp("act", ET[:, b, 4 * q:4 * q + 4, :], self.ps[pbk][:].rearrange("p (t c) -> p t c", t=4),
                        [self.b_ps[pbk]], [b_tab])
        P.barrier()
        if "tabs" in self.dbg:
            self.dump("tapsT", tapsT, (128, 4, 8, 128), BF16, [b_tab])
            self.dump("ET", ET, (128, 4, 16, 128), BF16, [b_tab])
            self.dump("CT", CT, (128, 16, 2, 8, 32), BF16, [b_tab])
            self.dump("kre", kre, (128, 16, 8), F32, [b_tab])
            self.dump("kim", kim, (128, 16, 8), F32, [b_tab])

    def run_ssm(self, l, s, mark, uTd, b_u, tapsT, ET, CT, kre, kim, nkim, b_tab, wglu, b_wglu):
        P = self.P
        S, NCH, NPC, NC8, LEV, PAD = self.S, self.NCH, self.NPC, self.NC8, self.LEV, self.PAD
        NB = S // 512
        bc = self.b_cst
        wa2 = Builder.Alloc(self, mark, OFF_WK + WK_SIZE - mark)
        ygT = wa2.get(BF16, (4, S))
        b_yg = Buf("ygT")
        Xb = [[wa2.get(F32, (2, PAD + NC8)) for _ in range(2)] for _ in range(2)]
        b_X = [[Buf() for _ in range(2)] for _ in range(2)]
        stage = [wa2.get(F32, (2, PAD + NC8)) for _ in range(2)]
        b_stage = [Buf() for _ in range(2)]
        for ri in range(2):
            self.memset("dve", stage[ri], 0.0, [b_stage[ri]])
        Hbf = [wa2.get(BF16, (4, NC8)) for _ in range(2)]
        b_H = Buf("Hbf")
        gx = [wa2.get(F32, (512,)) for _ in range(2)]
        g2 = [wa2.get(F32, (512,)) for _ in range(1)]
        gs = [wa2.get(F32, (512,)) for _ in range(1)]
        b_gx = [Buf() for _ in range(2)]
        b_g2 = [Buf() for _ in range(2)]
        b_gs = [Buf() for _ in range(2)]
        silcz = wa2.get(BF16, (S,))
        b_scz = Buf("silcz")
        sgl = [wa2.get(BF16, (512,)) for _ in range(2)]
        b_sgl = [Buf() for _ in range(2)]
        ygl = [wa2.get(BF16, (512,)) for _ in range(2)]
        b_ygl = [Buf() for _ in range(2)]
        for pp in range(2):
            for ri in range(2):
                self.memset("dve", Xb[pp][ri], 0.0, [b_X[pp][ri]])
        cnt = {"g": 0, "glu": 0}

        def u_job(fc):
            def f(views, buf):
                w = views[0]
                for pc in range(NPC):
                    pb = self.rot("C_p", [0, 1, 2, 3])
                    sl = slice(pc * 512, (pc + 1) * 512)
                    for kc in range(8):
                        self.mm(self.ps[pb][:], w[:, kc, :], self.hT[:, kc, sl], kc == 0, kc == 7, [buf, self.b_hT], [self.b_ps[pb]])
                    dst = uTd[:, fc, :].rearrange("p (j c) -> p c j", j=8)[:, pc * 64:(pc + 1) * 64, :]
                    self.cp("act", dst, self.ps[pb][:].rearrange("p (c j) -> p c j", j=8), [self.b_ps[pb]], [b_u])
            return f

        units = [(b_, hf_) for b_ in range(4) for hf_ in range(2)]

        def taps(b):
            for q in range(NB):
                for tau in range(8):
                    lo = max(512 * q, tau * NC8)
                    hi = 512 * (q + 1)
                    if lo >= hi:
                        continue
                    self.mm(self.ps[q][:, lo - 512 * q:hi - 512 * q], tapsT[:, b, tau, :],
                            uTd[:, b, lo - tau * NC8:hi - tau * NC8], tau == 0, False, [b_tab, b_u], [self.b_ps[q]])

        def s_stage(u):
            b, hf = units[u]
            for plh in range(2):
                pl = 2 * hf + plh
                for ri in range(2):
                    bk = 4 + plh * 2 + ri
                    for j in range(8):
                        er = (7 - j) * 2 + ri
                        self.mm(self.ps[bk][:, 0:NC8], ET[32 * pl:32 * pl + 32, b, er, :],
                                uTd[32 * pl:32 * pl + 32, b, j * NC8:(j + 1) * NC8], j == 0, j == 7,
                                [b_tab, b_u], [self.b_ps[bk]], tile_position=(32 * pl, 0))
                    self.cp("act", stage[ri][:, plh, PAD + 1:PAD + NC8], self.ps[bk][:, 0:NC8 - 1],
                            [self.b_ps[bk]], [b_stage[ri]])

        def ks(u):
            b, hf = units[u]
            for k in range(LEV):
                sh = 1 << k
                if k == 0:
                    src, bs = stage, b_stage
                else:
                    src, bs = Xb[k % 2], b_X[k % 2]
                dst, bd = Xb[(k + 1) % 2], b_X[(k + 1) % 2]
                first, second = [], []
                for plh in range(2):
                    Pg = 4 * b + 2 * hf + plh
                    kr = kre[:, Pg, k:k + 1]
                    ki = kim[:, Pg, k:k + 1]
                    nki = nkim[:, Pg, k:k + 1]
                    A_ = src[0][:, plh, PAD:PAD + NC8]
                    As = src[0][:, plh, PAD - sh:PAD + NC8 - sh]
                    B_ = src[1][:, plh, PAD:PAD + NC8]
                    Bs = src[1][:, plh, PAD - sh:PAD + NC8 - sh]
                    dR = dst[0][:, plh, PAD:PAD + NC8]
                    dI = dst[1][:, plh, PAD:PAD + NC8]
                    first.append((dR, As, kr, A_, [bs[0], b_tab], [bd[0]]))
                    first.append((dI, Bs, kr, B_, [bs[1], b_tab], [bd[1]]))
                    second.append((dR, Bs, nki, dR, [bs[1], b_tab], [bd[0]]))
                    second.append((dI, As, ki, dI, [bs[0], b_tab], [bd[1]]))
                for (o_, i0, sc, i1, r_, w_) in first + second:
                    self.stt(o_, i0, sc, i1, ALU.mult, ALU.add, r_, w_)
                if k == 0 and u + 1 < len(units):
                    s_stage(u + 1)
            fin = Xb[LEV % 2]
            bf = b_X[LEV % 2]
            for ri in range(2):
                self.cp("act", Hbf[ri][:, 2 * hf:2 * hf + 2, :], fin[ri][:, :, PAD:PAD + NC8], [bf[ri]], [b_H])

        def cross_gelu(b):
            for i in range(8):
                q = (i * NC8) // 512
                off = i * NC8 - 512 * q
                for pl in range(4):
                    for ri in range(2):
                        lastmm = (i == min(7, (512 * (q + 1) - 1) // NC8)) and pl == 3 and ri == 1
                        self.mm(self.ps[q][32 * pl:32 * pl + 32, off:off + NC8], CT[:, 4 * b + pl, ri, i, :], Hbf[ri][:, pl, :],
                                False, lastmm, [b_tab, b_H], [self.b_ps[q]], tile_position=(0, 32 * pl), skip_group_check=True)
            for q in range(NB):
                k = cnt["g"] % 2
                cnt["g"] += 1
                self.cp("act", gx[k], self.ps[q][:], [self.b_ps[q]], [b_gx[k]])
                self.act(g2[0], self.ps[q][:], AF.Square, [self.b_ps[q]], [b_g2[0]])
                self.act(g2[0], g2[0], AF.Identity, [b_g2[0], bc], [b_g2[0]], scale=0.044715 * 1.5957691216057308, bias=1.5957691216057308)
                self.tt("dve", g2[0], g2[0], gx[k], ALU.mult, [b_g2[0], b_gx[k]], [b_g2[0]])
                self.act(gs[0], g2[0], AF.Sigmoid, [b_g2[0]], [b_gs[0]])
                self.tt("dve", ygT[:, b, 512 * q:512 * (q + 1)], gx[k], gs[0], ALU.mult, [b_gx[k], b_gs[0]], [b_yg])

        def run_units():
            s_stage(0)
            for u in range(len(units)):
                b, hf = units[u]
                if hf == 0:
                    taps(b)
                ks(u)
                if hf == 1:
                    cross_gelu(b)

        st_ = {"glu": (wglu, b_wglu)}

        def cz_job(fo):
            def f(views, buf):
                w = views[0]
                wg, bg = st_["glu"]
                for pc in range(NPC):
                    pb = self.rot("C_p", [0, 1, 2, 3])
                    sl = slice(pc * 512, (pc + 1) * 512)
                    for kc in range(8):
                        self.mm(self.ps[pb][:], w[:, kc, :], self.hT[:, kc, sl], kc == 0, kc == 7, [buf, self.b_hT], [self.b_ps[pb]])
                    self.act(silcz[:, sl], self.ps[pb][:], AF.Silu, [self.b_ps[pb]], [b_scz])
                nj = 512 // NC8 if NC8 <= 512 else 1
                for q in range(NB):
                    k = cnt["glu"] % 2
                    cnt["glu"] += 1
                    pb = self.rot("C_p", [0, 1, 2, 3])
                    sl = slice(512 * q, 512 * (q + 1))
                    for fc in range(4):
                        self.mm(self.ps[pb][:], wg[:, fc, fo * 128:(fo + 1) * 128], ygT[:, fc, sl], fc == 0, fc == 3,
                                [bg, b_yg], [self.b_ps[pb]])
                    self.act(sgl[k], self.ps[pb][:], AF.Sigmoid, [self.b_ps[pb], bc], [b_sgl[k]],
                             bias=self.pk[:, PK_BG + fo:PK_BG + fo + 1])
                    self.tt("dve", ygl[k], ygT[:, fo, sl], sgl[k], ALU.mult, [b_yg, b_sgl[k]], [b_ygl[k]])
                    j0 = (512 * q) // NC8
                    ov = self.ycT[:, fo, :].rearrange("p (c j) -> p j c", j=8)[:, j0:j0 + nj, :]
                    cv = silcz.rearrange("p (c j) -> p j c", j=8)[:, j0:j0 + nj, :]
                    self.tt("dve", ov, ygl[k].rearrange("p (j c) -> p j c", j=nj), cv, ALU.mult, [b_ygl[k], b_scz], [self.b_yc])
            return f

        jobs = []
        for fc in range(4):
            jobs.append(([(self.win(l, C_CU + fc * 128, 128), 8, 128)], u_job(fc)))

        def run_blocks(views, buf):
            run_units()
        jobs.append(([], run_blocks))
        for fo in range(4):
            jobs.append(([(self.win(l, C_CZ + fo * 128, 128), 8, 128)], cz_job(fo)))
        self.run_jobs(jobs)
        self.dump("uTd", uTd, (128, 4, S), BF16, [b_u])
        self.dump("ygT", ygT, (128, 4, S), BF16, [b_yg])


def _pack_params(inp):
    L = NL
    f = lambda a: np.asarray(a, dtype=np.float32)
    pk = np.zeros((L, 128, NPK), np.float32)
    pka = np.zeros((L, 128, NPA), np.float32)
    pkc = np.zeros((L, 128, NPC_), np.float32)

    def sP(a):
        return a.reshape(16, 2, 64).transpose(1, 2, 0).reshape(128, 16)

    def sPm(a):
        return a.reshape(16, 2, 64, 16).transpose(1, 2, 0, 3).reshape(128, 256)
    for l in range(L):
        pk[l, :, PK_G:PK_G + 8] = f(inp["norm_g"])[l].reshape(8, 128).T
        pk[l, :, PK_ARE:PK_ARE + 16] = sP(f(inp["ssm_a_re"])[l])
        pk[l, :, PK_AIM:PK_AIM + 16] = sP(f(inp["ssm_a_im"])[l])
        pk[l, :, PK_LDT:PK_LDT + 16] = sP(np.broadcast_to(f(inp["ssm_log_dt"])[l][:, None], (32, 64)))
        pk[l, :, PK_D:PK_D + 4] = f(inp["ssm_d"])[l].reshape(4, 128).T
        pk[l, :, PK_BG:PK_BG + 4] = f(inp["ssm_b_glu"])[l].reshape(4, 128).T
        pk[l, :, PK_SBQ] = np.tile(f(inp["sb_q_norm"])[l], 2)
        pk[l, :, PK_SBK] = np.tile(f(inp["sb_k_norm"])[l], 2)
        pka[l, :, PA_Q:PA_Q + 128] = f(inp["ret_q_norm"])[l][None, :]
        pka[l, :, PA_K:PA_K + 128] = f(inp["ret_k_norm"])[l][None, :]
        pka[l, :, PA_O:PA_O + 1024] = f(inp["ret_out_norm"])[l][None, :]
        pkc[l, :, 0:256] = sPm(f(inp["ssm_b_re"])[l])
        pkc[l, :, 256:512] = sPm(f(inp["ssm_b_im"])[l])
        pkc[l, :, 512:768] = sPm(f(inp["ssm_c_re"])[l].transpose(0, 2, 1))
        pkc[l, :, 768:1024] = sPm(f(inp["ssm_c_im"])[l].transpose(0, 2, 1))
    return pk, pka, pkc


_NC_CACHE = {}


def _get_nc(S, NSEQ):
    key = (S, NSEQ)
    if key not in _NC_CACHE:
        _NC_CACHE[key] = Builder(S, NSEQ).build()
    return _NC_CACHE[key]


def kernel(**inputs):
    x = np.ascontiguousarray(np.asarray(inputs["x"], dtype=np.float32))
    B, S, _ = x.shape
    ncores = 8
    NSEQ = B // ncores
    pk, pka, pkc = _pack_params(inputs)
    cst = make_consts()
    shared = {
        "w_in": np.ascontiguousarray(np.asarray(inputs["w_in"], np.float32)),
        "proj_a": np.ascontiguousarray(np.asarray(inputs["proj_a"], np.float32)),
        "proj_b": np.ascontiguousarray(np.asarray(inputs["proj_b"], np.float32)),
        "proj_c": np.ascontiguousarray(np.asarray(inputs["proj_c"], np.float32)),
        "w_out": np.ascontiguousarray(np.asarray(inputs["w_out"], np.float32)),
        "w_glu": np.ascontiguousarray(np.asarray(inputs["ssm_w_glu"], np.float32)),
        "pk": pk, "pka": pka, "pkc": pkc, "cst": cst,
    }
    nc = _get_nc(S, NSEQ)
    in_maps = []
    for c in range(ncores):
        m = dict(shared)
        m["x"] = x[c * NSEQ:(c + 1) * NSEQ]
        in_maps.append(m)
    res = run_bass_kernel_spmd(nc, in_maps, core_ids=list(range(ncores)))
    out = np.concatenate([np.asarray(r["out"], dtype=np.float32) for r in res.results], axis=0)
    return out
```
